# Optimizing a Trainium2 kernel written in Bass

```python
import math
import jax
import jax.numpy as jnp
from jax import lax
import numpy as np

D_MODEL = 2048
BATCH = 8
SEQ = 2048
DEPTH = 1

HG_WIDTH = D_MODEL // 2
HG_HEAD_DIM = 128
HG_HEADS = HG_WIDTH // HG_HEAD_DIM
HG_CHUNK = 64
DA_WIDTH = D_MODEL - HG_WIDTH
DA_HEAD_DIM = 64
DA_HEADS = DA_WIDTH // (2 * DA_HEAD_DIM)
DA_Q_BLOCK = 128
N_BUCKETS = 32
MAX_DISTANCE = 128
D_FF = -(-(8 * D_MODEL) // (3 * 256)) * 256
IN_COLS = 5 * HG_WIDTH + 3 * DA_WIDTH
EPS = 1e-6

kernel_name = 'hybrid_hgrn2_diffattn_block'


def rms_norm(x, w):
    xf = x.astype(jnp.float32)
    y = xf * lax.rsqrt(jnp.mean(xf * xf, axis=-1, keepdims=True) + EPS)
    return (y * w.astype(jnp.float32)).astype(x.dtype)


def rel_bucket(rel):
    nb = N_BUCKETS // 2
    max_exact = nb // 2
    ret = jnp.where(rel > 0, nb, 0)
    n = jnp.abs(rel)
    nf = jnp.maximum(n, 1).astype(jnp.float32)
    large = max_exact + (jnp.log(nf / max_exact) / math.log(MAX_DISTANCE / max_exact)
                         * (nb - max_exact)).astype(jnp.int32)
    large = jnp.minimum(large, nb - 1)
    return ret + jnp.where(n < max_exact, n, large)


def hgrn2_scan(q, k, v, logf):
    B, H, S, dk = q.shape
    dv = v.shape[-1]
    C = HG_CHUNK
    N = S // C

    def to_chunks(t):
        return jnp.moveaxis(t.astype(jnp.float32).reshape(B, H, N, C, t.shape[-1]), 2, 0)

    qc, kc, vc, gc = to_chunks(q), to_chunks(k), to_chunks(v), to_chunks(logf)
    mask = jnp.tril(jnp.ones((C, C), dtype=bool))[:, :, None]

    def step(s_prev, inp):
        qi, ki, vi, gi = inp
        b = jnp.cumsum(gi, axis=2)
        inter = jnp.einsum('bhtk,bhkv->bhtv', qi * jnp.exp(b), s_prev)
        diff = b[:, :, :, None, :] - b[:, :, None, :, :]
        decay = jnp.exp(jnp.where(mask, diff, -jnp.inf))
        scores = jnp.einsum('bhtk,bhsk,bhtsk->bhts', qi, ki, decay)
        o = inter + jnp.einsum('bhts,bhsv->bhtv', scores, vi)
        b_last = b[:, :, -1:, :]
        s_new = (jnp.exp(b_last[:, :, 0, :])[..., None] * s_prev
                 + jnp.einsum('bhsk,bhsv->bhkv', ki * jnp.exp(b_last - b), vi))
        return s_new, o

    s0 = jnp.zeros((B, H, dk, dv), jnp.float32)
    _, oc = lax.scan(step, s0, (qc, kc, vc, gc))
    return jnp.moveaxis(oc, 0, 2).reshape(B, H, S, dv)


def diff_attention(q1, q2, k1, k2, v, lam, bias_table):
    B, H, S, d = q1.shape
    Q = DA_Q_BLOCK
    N = S // Q
    scale = d ** -0.5
    kpos = jnp.arange(S, dtype=jnp.int32)

    def blk(inp):
        qb1, qb2, qpos = inp
        bias = bias_table[rel_bucket(kpos[None, :] - qpos[:, None])]
        bias = jnp.moveaxis(bias.astype(jnp.float32), -1, 0)[None]
        s1 = jnp.einsum('bhqd,bhkd->bhqk', qb1, k1).astype(jnp.float32) * scale + bias
        s2 = jnp.einsum('bhqd,bhkd->bhqk', qb2, k2).astype(jnp.float32) * scale + bias
        p = jax.nn.softmax(s1, axis=-1) - lam * jax.nn.softmax(s2, axis=-1)
        return jnp.einsum('bhqk,bhkv->bhqv', p.astype(v.dtype), v)

    qb1 = jnp.moveaxis(q1.reshape(B, H, N, Q, d), 2, 0)
    qb2 = jnp.moveaxis(q2.reshape(B, H, N, Q, d), 2, 0)
    qpos = jnp.arange(S, dtype=jnp.int32).reshape(N, Q)
    o = lax.map(blk, (qb1, qb2, qpos))
    return jnp.moveaxis(o, 0, 2).reshape(B, H, S, v.shape[-1])


def setup_inputs(seed: int = 0):
    key = jax.random.key(seed)
    ks = jax.random.split(key, 17)
    f32 = jnp.float32

    def nrm(k, shape, scale):
        return jax.random.normal(k, shape, f32) * scale

    def gain(k, shape):
        return 1.0 + 0.02 * jax.random.normal(k, shape, f32)

    return {
        'x': nrm(ks[0], (BATCH, SEQ, D_MODEL), 1.0),
        'norm1_w': gain(ks[1], (DEPTH, D_MODEL)),
        'w_in': nrm(ks[2], (DEPTH, D_MODEL, IN_COLS), D_MODEL ** -0.5),
        'hg_lb_logits': nrm(ks[3], (2, DEPTH + 1, HG_WIDTH), 0.5),
        'hg_onorm_w': gain(ks[4], (DEPTH, HG_HEAD_DIM)),
        'lambda_q1': nrm(ks[5], (DEPTH, DA_HEAD_DIM), 0.1),
        'lambda_k1': nrm(ks[6], (DEPTH, DA_HEAD_DIM), 0.1),
        'lambda_q2': nrm(ks[7], (DEPTH, DA_HEAD_DIM), 0.1),
        'lambda_k2': nrm(ks[8], (DEPTH, DA_HEAD_DIM), 0.1),
        'da_subln_w': gain(ks[9], (DEPTH, 2 * DA_HEAD_DIM)),
        'rel_bias': nrm(ks[10], (N_BUCKETS, DA_HEADS), 0.5),
        'w_out': nrm(ks[11], (DEPTH, D_MODEL, D_MODEL), D_MODEL ** -0.5),
        'norm2_w': gain(ks[12], (DEPTH, D_MODEL)),
        'w_gate': nrm(ks[13], (DEPTH, D_MODEL, D_FF), D_MODEL ** -0.5),
        'w_up': nrm(ks[14], (DEPTH, D_MODEL, D_FF), D_MODEL ** -0.5),
        'w_down': nrm(ks[15], (DEPTH, D_FF, D_MODEL), D_FF ** -0.5),
        'final_norm_w': gain(ks[16], (D_MODEL,)),
    }


def reference(x, norm1_w, w_in, hg_lb_logits, hg_onorm_w, lambda_q1, lambda_k1,
              lambda_q2, lambda_k2, da_subln_w, rel_bias, w_out, norm2_w,
              w_gate, w_up, w_down, final_norm_w):
    B, S, _ = x.shape
    splits = [HG_WIDTH, 2 * HG_WIDTH, 3 * HG_WIDTH, 4 * HG_WIDTH, 5 * HG_WIDTH,
              5 * HG_WIDTH + DA_WIDTH, 5 * HG_WIDTH + 2 * DA_WIDTH]
    lower_bounds = jnp.cumsum(jax.nn.softmax(hg_lb_logits.astype(jnp.float32), axis=1), axis=1)

    def heads(t, dh):
        return t.reshape(B, S, -1, dh).transpose(0, 2, 1, 3)

    h = x
    for l in range(DEPTH):
        u = rms_norm(h, norm1_w[l])
        proj = u @ w_in[l]
        hq, hi, hf_fwd, hf_bwd, hg, dq, dk, dv = jnp.split(proj, splits, axis=-1)

        q = heads(jax.nn.silu(hq), HG_HEAD_DIM)
        vin = heads(hi, HG_HEAD_DIM)
        lb_f = lower_bounds[0, l]
        lb_b = lower_bounds[1, l]
        f_f = lb_f + (1.0 - lb_f) * jax.nn.sigmoid(hf_fwd.astype(jnp.float32))
        f_b = lb_b + (1.0 - lb_b) * jax.nn.sigmoid(hf_bwd.astype(jnp.float32))
        f_f = heads(f_f, HG_HEAD_DIM)
        f_b = heads(f_b, HG_HEAD_DIM)
        o_fwd = hgrn2_scan(q, 1.0 - f_f, vin, jnp.log(f_f))
        o_bwd = jnp.flip(hgrn2_scan(jnp.flip(q, 2), jnp.flip(1.0 - f_b, 2),
                                    jnp.flip(vin, 2), jnp.flip(jnp.log(f_b), 2)), 2)
        o_hg = (o_fwd + o_bwd).astype(x.dtype).transpose(0, 2, 1, 3)
        o_hg = rms_norm(o_hg, hg_onorm_w[l]) * jax.nn.silu(hg.reshape(B, S, HG_HEADS, HG_HEAD_DIM))
        o_hg = o_hg.reshape(B, S, HG_WIDTH)

        dq5 = dq.reshape(B, S, DA_HEADS, 2, DA_HEAD_DIM).transpose(0, 2, 3, 1, 4)
        dk5 = dk.reshape(B, S, DA_HEADS, 2, DA_HEAD_DIM).transpose(0, 2, 3, 1, 4)
        dvh = heads(dv, 2 * DA_HEAD_DIM)
        lam_init = 0.8 - 0.6 * math.exp(-0.3 * l)
        lam = (jnp.exp(jnp.sum(lambda_q1[l].astype(jnp.float32) * lambda_k1[l].astype(jnp.float32)))
               - jnp.exp(jnp.sum(lambda_q2[l].astype(jnp.float32) * lambda_k2[l].astype(jnp.float32)))
               + lam_init)
        o_da = diff_attention(dq5[:, :, 0], dq5[:, :, 1], dk5[:, :, 0], dk5[:, :, 1],
                              dvh, lam, rel_bias)
        o_da = rms_norm(o_da.transpose(0, 2, 1, 3), da_subln_w[l]) * (1.0 - lam_init)
        o_da = o_da.reshape(B, S, DA_WIDTH)

        h = h + jnp.concatenate([o_hg, o_da], axis=-1) @ w_out[l]

        u2 = rms_norm(h, norm2_w[l])
        h = h + (jax.nn.silu(u2 @ w_gate[l]) * (u2 @ w_up[l])) @ w_down[l]

    return rms_norm(h, final_norm_w)
```

```python
import math
from contextlib import ExitStack

import numpy as np
import concourse.bass as bass
import concourse.mybir as mybir
from concourse.bass_utils import run_bass_kernel_spmd

F32 = mybir.dt.float32
BF16 = mybir.dt.bfloat16
AF = mybir.ActivationFunctionType
ALU = mybir.AluOpType

ENGS = ("pe", "act", "dve", "pool", "sp")
S_LEN = 2048
DM = 2048
DFF = 5632
NJ = DFF // 128
EPS = 1e-6
LAM_INIT = 0.8 - 0.6 * math.exp(-0.3 * 0)


class Buf:
    def __init__(self, name):
        self.name = name
        self.w = None
        self.r = {}
        self.dsem = None
        self.dcnt = 0


class _Rec:
    def __init__(self):
        self.calls = []

    def __getattr__(self, name):
        def f(*a, **k):
            self.calls.append((name, a, k))
            return self
        return f


class Sched:
    def __init__(self, nc, es):
        self.nc = nc
        self.es = es
        self.prog = {e: [] for e in ENGS}
        self.sem = {}
        self.cnt = {}
        for e in ("pe", "act", "dve", "pool"):
            self.sem[e] = es.enter_context(nc.semaphore("sem_" + e))
            self.cnt[e] = 0
        self.known = {e: {} for e in ENGS}
        self.pe_pending = False
        self.dbufs = []

    def _tokens(self, reads, writes):
        toks = []
        for b in reads:
            if b.w is not None:
                toks.append(b.w)
        for b in writes:
            if b.w is not None:
                toks.append(b.w)
            toks.extend(b.r.values())
        return toks

    def _wait(self, e, toks):
        kn = self.known[e]
        need = {}
        for (sem, v) in toks:
            if e == "pe" and sem is self.sem["pe"]:
                continue
            k = id(sem)
            if kn.get(k, 0) < v and need.get(k, (None, 0))[1] < v:
                need[k] = (sem, v)
        for k, (sem, v) in need.items():
            kn[k] = v
            self.prog[e].append(lambda eng, sem=sem, v=v: eng.wait_ge(sem, v))

    def _record(self, tok, reads, writes):
        k = id(tok[0])
        for b in reads:
            if b.r.get(k, (None, 0))[1] < tok[1]:
                b.r[k] = tok
        for b in writes:
            b.w = tok
            b.r = {}

    def op(self, e, fn, reads=(), writes=(), inc=True):
        rec = _Rec()
        fn(rec)
        assert len(rec.calls) == 1
        name, a, k = rec.calls[0]
        fn = lambda eng, name=name, a=a, k=k: getattr(eng, name)(*a, **k)
        self._wait(e, self._tokens(reads, writes))
        sem = self.sem[e]
        if inc:
            self.cnt[e] += 1
            tok = (sem, self.cnt[e])
            self.prog[e].append(lambda eng, fn=fn, sem=sem: fn(eng).then_inc(sem, 1))
            if e == "pe":
                self.pe_pending = False
        else:
            assert e == "pe"
            tok = (sem, self.cnt[e] + 1)
            self.prog[e].append(lambda eng, fn=fn: fn(eng))
            self.pe_pending = True
        self._record(tok, reads, writes)

    def dma(self, q, out_ap, in_ap, reads=(), writes=(), slot=None, concurrent=False, **kw):
        if slot is None:
            slot = writes[0] if writes else reads[0]
        if slot.dsem is None:
            slot.dsem = self.es.enter_context(self.nc.semaphore("dsem_" + slot.name))
            self.dbufs.append(slot)
        toks = self._tokens(reads, writes)
        if concurrent:
            toks = [t for t in toks if t[0] is not slot.dsem]
        self._wait(q, toks)
        slot.dcnt += 16
        tok = (slot.dsem, slot.dcnt)
        self.prog[q].append(
            lambda eng, o=out_ap, i=in_ap, s=slot.dsem, kw=kw: eng.dma_start(out=o, in_=i, **kw).then_inc(s, 16))
        self._record(tok, reads, writes)

    def barrier(self):
        assert not self.pe_pending
        toks = [(self.sem[e], self.cnt[e]) for e in ("pe", "act", "dve", "pool") if self.cnt[e] > 0]
        toks += [(b.dsem, b.dcnt) for b in self.dbufs if b.dcnt > 0]
        for e in ENGS:
            self._wait(e, toks)

    def emit(self):
        assert not self.pe_pending
        nc = self.nc
        engmap = {"pe": "tensor", "act": "scalar", "dve": "vector", "pool": "gpsimd", "sp": "sync"}
        with nc.Block() as block:
            for e in ENGS:
                prog = self.prog[e]

                def body(eng, prog=prog):
                    for f in prog:
                        f(eng)
                getattr(block, engmap[e])(body)


class Arena:
    def __init__(self, ap_bf16):
        self.t = ap_bf16
        self.off = 0
        self.cap = ap_bf16.shape[1]

    def alloc(self, free_shape, dt):
        n = 1
        for s in free_shape:
            n *= s
        nel = n * 2 if dt == F32 else n
        nel = (nel + 15) // 16 * 16
        assert self.off + nel <= self.cap, ("arena overflow", self.off, nel, self.cap)
        v = self.t[:, self.off:self.off + (n * 2 if dt == F32 else n)]
        self.off += nel
        if dt == F32:
            v = v.bitcast(F32)
        if len(free_shape) == 2:
            v = v.rearrange("p (a b) -> p a b", a=free_shape[0])
        elif len(free_shape) == 3:
            v = v.rearrange("p (a b c) -> p a b c", a=free_shape[0], b=free_shape[1])
        return v


class _Stop(Exception):
    pass


def build_nc(stop=None):
    holder = {}
    try:
        _build(holder, stop)
    except _Stop:
        pass
    return holder["nc"]


def _build(holder, stop):
    nc = bass.Bass("TRN2", target_bir_lowering=False)
    holder["nc"] = nc
    dt_in = lambda name, shape: nc.dram_tensor(name, shape, F32, kind="ExternalInput").ap()
    x = dt_in("x", [S_LEN, DM])
    w_in = dt_in("w_in", [DM, 8192])
    w_out = dt_in("w_out", [DM, DM])
    w_gate = dt_in("w_gate", [DM, DFF])
    w_up = dt_in("w_up", [DM, DFF])
    w_down = dt_in("w_down", [DFF, DM])
    norm1_w = dt_in("norm1_w", [DM])
    norm2_w = dt_in("norm2_w", [DM])
    final_w = dt_in("final_w", [DM])
    lb_logits = dt_in("lb_logits", [2, 2, 1024])
    onorm_w = dt_in("onorm_w", [128])
    subln_w = dt_in("subln_w", [128])
    lam_in = dt_in("lam_in", [4, 64])
    bias_near = dt_in("bias_near", [128, 8 * 384])
    bias_far = dt_in("bias_far", [16])
    hmask = dt_in("hmask", [128, 2 * 512])
    out = nc.dram_tensor("out", [S_LEN, DM], F32, kind="ExternalOutput").ap()
    ycat = nc.dram_tensor("ycat", [DM, S_LEN], BF16, kind="Internal").ap()
    wo_s = nc.dram_tensor("wo_s", [4, 2, 128, 8, 512], BF16, kind="Internal").ap()
    wgu_s = nc.dram_tensor("wgu_s", [22, 2, 128, 8, 2, 256], BF16, kind="Internal").ap()
    wd_s = nc.dram_tensor("wd_s", [4, 128, NJ, 512], BF16, kind="Internal").ap()
    if stop is not None:
        dbg_bf = nc.dram_tensor("dbg_bf", [128, 16 * S_LEN], BF16, kind="ExternalOutput").ap()
        dbg_f = nc.dram_tensor("dbg_f", [128, 8 * S_LEN], F32, kind="ExternalOutput").ap()

    with ExitStack() as es:
        S = Sched(nc, es)
        arena_t = es.enter_context(nc.sbuf_tensor("arena", [128, 103 * 1024], BF16))
        A = Arena(arena_t[:])
        ps = es.enter_context(nc.psum_tensor("ps", [128, 4096], F32))
        PB = [Buf("psb%d" % i) for i in range(8)]

        def psb(i, n=1):
            return ps[:, i * 512:(i + n) * 512]

        def psb16(i):
            return ps[:, i * 512:(i + 1) * 512].bitcast(BF16)

        ident = A.alloc([128], BF16); B_const = Buf("const")
        ones_bf = A.alloc([128], BF16)
        maskF = A.alloc([512], BF16)
        maskB = A.alloc([512], BF16)
        lbt = A.alloc([2, 2, 8], F32)
        lbv = A.alloc([2, 8], F32)
        omlv = A.alloc([2, 8], F32)
        onw = A.alloc([1], F32)
        sublnbc = A.alloc([128], F32)
        lamt = A.alloc([4, 64], F32)
        lams = A.alloc([8], F32)
        lamj = A.alloc([64], F32)
        efar = A.alloc([16], F32)
        mhalfc = A.alloc([1], F32)
        epsc = A.alloc([1], F32)
        Eall = A.alloc([8, 384], BF16)
        Etmp = None

        S.op("pool", lambda e: e.memset(ones_bf, 1.0), writes=[B_const])
        S.op("pool", lambda e: e.memset(mhalfc, -0.5), writes=[B_const])
        S.op("pool", lambda e: e.memset(epsc, EPS), writes=[B_const])
        S.op("pool", lambda e: e.affine_select(ident, ones_bf, [[-1, 128]], ALU.is_equal, 0.0, base=0, channel_multiplier=1),
             writes=[B_const])
        B_ld = Buf("cld")
        B_ldp = Buf("cldp")
        S.dma("pool", maskF, hmask[:, 0:512], writes=[B_ldp])
        S.dma("pool", maskB, hmask[:, 512:1024], writes=[B_ldp])
        for d in range(2):
            for sl in range(2):
                S.dma("sp", lbt[:, d, sl, :], lb_logits[d, sl].rearrange("(h p) -> p h", p=128), writes=[B_ld],
                      allow_slow_non_contiguous=True, concurrent=True)
        S.dma("sp", onw, onorm_w.rearrange("(p o) -> p o", o=1), writes=[B_ld], allow_slow_non_contiguous=True, concurrent=True)
        S.dma("sp", sublnbc, subln_w.partition_broadcast(128), writes=[B_ld], concurrent=True)
        S.dma("sp", lamt.rearrange("p a b -> p (a b)"), lam_in.rearrange("a b -> (a b)").partition_broadcast(128), writes=[B_ld], concurrent=True)
        S.dma("sp", efar, bias_far.partition_broadcast(128), writes=[B_ld], concurrent=True)
        B_c2 = Buf("const2")
        S.op("dve", lambda e: e.tensor_tensor(lbv, lbt[:, :, 0, :], lbt[:, :, 1, :], ALU.subtract), reads=[B_ld], writes=[B_c2])
        S.op("act", lambda e: e.activation(lbv, lbv, AF.Sigmoid), reads=[B_c2], writes=[B_c2])
        S.op("dve", lambda e: e.tensor_scalar(omlv, lbv, -1.0, 1.0, ALU.mult, ALU.add), reads=[B_c2], writes=[B_c2])
        S.op("dve", lambda e: e.tensor_scalar(sublnbc, sublnbc, 1.0 - LAM_INIT, None, ALU.mult), reads=[B_ld], writes=[B_c2])
        S.op("dve", lambda e: e.tensor_tensor(lamj, lamt[:, 0, :], lamt[:, 1, :], ALU.mult), reads=[B_ld], writes=[B_c2])
        S.op("dve", lambda e: e.tensor_reduce(lams[:, 0:1], lamj, mybir.AxisListType.X, ALU.add), reads=[B_c2], writes=[B_c2])
        S.op("dve", lambda e: e.tensor_tensor(lamj, lamt[:, 2, :], lamt[:, 3, :], ALU.mult), reads=[B_ld, B_c2], writes=[B_c2])
        S.op("dve", lambda e: e.tensor_reduce(lams[:, 1:2], lamj, mybir.AxisListType.X, ALU.add), reads=[B_c2], writes=[B_c2])
        S.op("act", lambda e: e.activation(lams[:, 2:4], lams[:, 0:2], AF.Exp), reads=[B_c2], writes=[B_c2])
        S.op("dve", lambda e: e.tensor_tensor(lams[:, 4:5], lams[:, 2:3], lams[:, 3:4], ALU.subtract), reads=[B_c2], writes=[B_c2])
        S.op("dve", lambda e: e.tensor_scalar(lams[:, 5:6], lams[:, 4:5], -1.0, -LAM_INIT, ALU.mult, ALU.add), reads=[B_c2], writes=[B_c2])
        neglam = lams[:, 5:6]
        S.op("act", lambda e: e.activation(efar, efar, AF.Exp), reads=[B_ld], writes=[B_c2])
        CONST = [B_const, B_c2, B_ld, B_ldp]
        base_off = A.off

        B_dbg = Buf("dbg")

        def dump(ap, col0, rd):
            tgt = dbg_f if ap.dtype == F32 else dbg_bf
            S.dma("sp", tgt[:, col0:col0 + ap.shape[1]], ap, reads=rd, writes=[B_dbg], slot=B_dbg)

        def check_stop(tag, fn=None):
            if stop == tag:
                if fn is not None:
                    fn()
                S.barrier()
                S.emit()
                raise _Stop()

        uT = A.alloc([16, S_LEN], BF16); B_uT = [Buf("uTa"), Buf("uTb")]
        NSLAB = 5
        slabs = [A.alloc([16, 128], BF16) for _ in range(NSLAB)]; B_slab = [Buf("slab%d" % i) for i in range(NSLAB)]
        hd_off = A.off
        mskf = A.alloc([S_LEN], BF16); B_msk = Buf("msk")
        S.op("pool", lambda e: e.memset(mskf, 1.0), writes=[B_msk])
        S.op("pool", lambda e: e.memset(mskf[:, 0::64], 0.0), writes=[B_msk])

        def load_slab(i, col0):
            S.dma("pool", slabs[i], w_in[:, col0:col0 + 128].rearrange("(c p) n -> p c n", p=128), writes=[B_slab[i]])

        p2_off = A.off
        w1bc = A.alloc([DM], F32); B_w1 = Buf("w1bc")
        NX = 4
        xts = [A.alloc([DM], F32) for _ in range(NX)]; B_xt = [Buf("xt%d" % i) for i in range(NX)]
        ubs = [A.alloc([DM], BF16) for _ in range(3)]; B_ub = [Buf("ub%d" % i) for i in range(3)]
        st1 = A.alloc([16, 4], F32)
        Etmp = A.alloc([8 * 384], F32); B_Et = Buf("Etmp")
        S.dma("sp", Etmp, bias_near, writes=[B_Et])
        S.op("act", lambda e: e.activation(Eall.rearrange("p a b -> p (a b)"), Etmp, AF.Exp), reads=[B_Et], writes=[B_c2])
        S.dma("sp", w1bc, norm1_w.partition_broadcast(128), writes=[B_w1])

        def rms_token_tile(src, B_src, wbc, B_wbc, dst, B_dst, stc, B_st):
            jv = dst if dst.dtype == BF16 else dst.bitcast(BF16)[:, 0:DM]
            S.op("act", lambda e: e.activation(jv, src, AF.Square, accum_out=stc[:, 0:1]), reads=[B_src], writes=[B_dst, B_st])
            S.op("pool", lambda e: e.tensor_scalar(stc[:, 1:2], stc[:, 0:1], 1.0 / DM, EPS, ALU.mult, ALU.add), reads=[B_st], writes=[B_st])
            S.op("pool", lambda e: e.tensor_tensor(stc[:, 2:3], stc[:, 1:2], mhalfc, ALU.pow), reads=[B_st, B_const], writes=[B_st])
            S.op("dve", lambda e: e.scalar_tensor_tensor(dst, src, stc[:, 2:3], wbc, ALU.mult, ALU.mult),
                 reads=[B_src, B_st, B_wbc], writes=[B_dst])

        def transpose_tile_to(srcb, B_src, dstT, B_dstT, t0, banks, evac_engs):
            for g in range(2):
                bk = banks[g]
                for j in range(8):
                    c = g * 8 + j
                    S.op("pe", lambda e, c=c, j=j, bk=bk: e.transpose(psb16(bk)[:, j * 128:(j + 1) * 128], srcb[:, c * 128:(c + 1) * 128], ident),
                         reads=[B_src, B_const], writes=[PB[bk]], inc=(j == 7))
                dv = dstT[:, g * 8:(g + 1) * 8, t0:t0 + 128]
                sv = psb16(bk).rearrange("p (a b) -> p a b", a=8)
                if evac_engs[g] == "act":
                    S.op("act", lambda e, dv=dv, sv=sv: e.activation(dv, sv, AF.Copy), reads=[PB[bk]], writes=[B_dstT[g]])
                else:
                    S.op("dve", lambda e, dv=dv, sv=sv: e.tensor_copy(dv, sv), reads=[PB[bk]], writes=[B_dstT[g]])

        for tt in range(NX):
            S.dma("pool", xts[tt], x[tt * 128:(tt + 1) * 128, :], writes=[B_xt[tt]])
        if stop != "da1":
            for i in range(5):
                load_slab(i, [0, 1024, 4096, 2048, 3072][i])
        for tt in range(17):
            if tt < 16:
                s_ = tt % NX
                B_st = Buf("st1_%d" % tt)
                rms_token_tile(xts[s_], B_xt[s_], w1bc, B_w1, ubs[tt % 3], B_ub[tt % 3], st1[:, tt, :], B_st)
                if tt + NX < 16:
                    S.dma("pool", xts[s_], x[(tt + NX) * 128:(tt + NX + 1) * 128, :], writes=[B_xt[s_]])
            if tt >= 1:
                t_ = tt - 1
                transpose_tile_to(ubs[t_ % 3], B_ub[t_ % 3], uT, B_uT, t_ * 128, (0, 1) if t_ % 2 == 0 else (2, 3), ("act", "dve"))

        check_stop("p1", lambda: [dump(uT[:, fc, :], fc * S_LEN, B_uT) for fc in range(16)])
        A.off = p2_off

        pj_rot = [0]

        def proj(i, evac):
            for tb in range(4):
                bk = pj_rot[0] % 4
                pj_rot[0] += 1
                for fc in range(16):
                    S.op("pe", lambda e, fc=fc, tb=tb, bk=bk: e.matmul(psb(bk), lhsT=slabs[i][:, fc, :], rhs=uT[:, fc, tb * 512:(tb + 1) * 512],
                                                                      start=(fc == 0), stop=(fc == 15)),
                         reads=[B_slab[i]] + B_uT, writes=[PB[bk]], inc=(fc == 15))
                evac(tb, bk)

        def tsl(tb):
            return slice(tb * 512, (tb + 1) * 512)

        q32 = A.alloc([S_LEN], BF16); B_q = Buf("q32")
        vtok = A.alloc([16, 128], BF16); B_vtok = Buf("vtok")
        sgate = A.alloc([S_LEN], BF16); B_sg = Buf("sgate")
        O32 = A.alloc([S_LEN], F32); B_O = Buf("O32")
        R = A.alloc([4, S_LEN], F32); B_R = [Buf("R%d" % i) for i in range(4)]
        RA, RB, RC, RD = R[:, 0, :], R[:, 1, :], R[:, 2, :], R[:, 3, :]
        Xv = A.alloc([2 * S_LEN], F32); B_X = Buf("X")
        Dq = A.alloc([1024], F32); B_Dq = Buf("Dq")
        qtTs = [A.alloc([S_LEN], BF16) for _ in range(2)]; B_qts = [Buf("qtT%d" % i) for i in range(2)]
        ktTs = [A.alloc([S_LEN], BF16) for _ in range(2)]; B_kts = [Buf("ktT%d" % i) for i in range(2)]
        ktok = A.alloc([16, 128], BF16); B_ktok = Buf("ktok")
        ATm = A.alloc([16, 128], BF16); B_AT = Buf("ATm")
        vT = ATm.rearrange("p a b -> p (a b)"); B_vT = B_AT
        Sst = A.alloc([128, 32], BF16); B_Sst = Buf("Sst")
        Dts = [A.alloc([32], F32) for _ in range(2)]; B_Dt = [Buf("Dt%d" % i) for i in range(2)]
        Dscs = [A.alloc([32], F32) for _ in range(2)]; B_Dsc = [Buf("Dsc%d" % i) for i in range(2)]
        B_ycat = Buf("ycat_hbm")
        hg_end = A.off

        HG_COL = [0, 1024, 4096, 2048, 3072]
        B_conv = [Buf("conv%d" % i) for i in range(8)]
        conv_list = []
        for cb in range(4):
            for hf in range(2):
                conv_list.append((wo_s[cb, hf],
                                  w_out[hf * 1024:(hf + 1) * 1024, cb * 512:(cb + 1) * 512].rearrange("(fc p) n -> p fc n", p=128)))
        for pr in range(22):
            for g, wsrc in enumerate((w_gate, w_up)):
                for hf in range(2):
                    conv_list.append((wgu_s[pr, hf, :, :, g, :],
                                      wsrc[hf * 1024:(hf + 1) * 1024, pr * 256:(pr + 1) * 256].rearrange("(fc p) n -> p fc n", p=128)))
        for cb in range(4):
            for (j0, j1) in ((0, 16), (16, 32), (32, 44)):
                conv_list.append((wd_s[cb, :, j0:j1, :],
                                  w_down[j0 * 128:j1 * 128, cb * 512:(cb + 1) * 512].rearrange("(j p) n -> p j n", p=128)))
        conv_pos = [0]

        def emit_conv(n):
            for _ in range(n):
                if conv_pos[0] >= len(conv_list):
                    return
                dst_, src_ = conv_list[conv_pos[0]]
                S.dma("pool", dst_, src_, writes=[B_conv[conv_pos[0] % 8]])
                conv_pos[0] += 1

        def hg_proj(h, i, evac, defer=False):
            proj(i, evac)
            if not defer:
                post_proj(h, i)

        def post_proj(h, i):
            if h + 1 < 8:
                load_slab(i, HG_COL[i] + (h + 1) * 128)
            elif i < 3 and stop is None:
                load_slab(i, [5120, 6144, 7168][i])
            if stop is None:
                emit_conv(3)

        def projA(h, d):
            def ev_f(tb, bk):
                S.op("act", lambda e: e.activation(RA[:, tsl(tb)], psb(bk), AF.Sigmoid), reads=[PB[bk]], writes=[B_R[0]])
                S.op("act", lambda e: e.activation(RB[:, tsl(tb)], psb(bk), AF.Sigmoid, scale=-1.0), reads=[PB[bk]], writes=[B_R[1]])
            hg_proj(h, 3 + d, ev_f, defer=True)

        def projQ(h):
            hg_proj(h, 0, lambda tb, bk: S.op("act", lambda e: e.activation(q32[:, tsl(tb)], psb(bk), AF.Silu), reads=[PB[bk]], writes=[B_q]))

        def projI(h):
            hg_proj(h, 1, lambda tb, bk: S.op("act", lambda e: e.activation(vT[:, tsl(tb)], psb(bk), AF.Copy), reads=[PB[bk]], writes=[B_vT]))
            for g in range(2):
                bk = 4 + g
                for j in range(8):
                    blk = g * 8 + j
                    S.op("pe", lambda e, blk=blk, j=j, bk=bk: e.transpose(psb16(bk)[:, j * 128:(j + 1) * 128], vT[:, blk * 128:(blk + 1) * 128], ident),
                         reads=[B_vT, B_const], writes=[PB[bk]], inc=(j == 7))
                S.op("dve", lambda e, g=g, bk=bk: e.tensor_copy(vtok[:, g * 8:(g + 1) * 8, :], psb16(bk).rearrange("p (a b) -> p a b", a=8)),
                     reads=[PB[bk]], writes=[B_vtok])

        def projG(h):
            hg_proj(h, 2, lambda tb, bk: S.op("act", lambda e: e.activation(sgate[:, tsl(tb)], psb(bk), AF.Silu), reads=[PB[bk]], writes=[B_sg]))

        def chainA_early(h, d):
            fwd = (d == 0)
            oml = omlv[:, d, h:h + 1]
            lb = lbv[:, d, h:h + 1]
            S.op("pool", lambda e: e.tensor_scalar(RA, RA, oml, lb, ALU.mult, ALU.add), reads=[B_R[0], B_c2], writes=[B_R[0]])
            post_proj(h, 3 + d)
            S.op("act", lambda e: e.activation(RA, RA, AF.Ln), reads=[B_R[0]], writes=[B_R[0]])
            if fwd:
                S.op("dve", lambda e: e.tensor_tensor_scan(RC, mskf, RA, 0.0, ALU.mult, ALU.add), reads=[B_R[0], B_msk], writes=[B_R[2]])
            else:
                S.op("dve", lambda e: e.tensor_tensor_scan(RC[:, ::-1], mskf, RA[:, ::-1], 0.0, ALU.mult, ALU.add),
                     reads=[B_R[0], B_msk], writes=[B_R[2]])
            S.op("act", lambda e: e.activation(RA, RC, AF.Exp), reads=[B_R[2]], writes=[B_R[0]])
            S.op("act", lambda e: e.activation(RD, RC, AF.Exp, scale=-1.0), reads=[B_R[2]], writes=[B_R[3]])
            dsrc = RA[:, 63::64] if fwd else RA[:, 0::64]
            S.op("dve", lambda e: e.tensor_copy(Dts[d], dsrc), reads=[B_R[0]], writes=[B_Dt[d]])
            S.op("dve", lambda e: e.tensor_copy(Dscs[d], dsrc), reads=[B_R[0]], writes=[B_Dsc[d]])
            zc = 0 if fwd else 31
            S.op("dve", lambda e: e.memset(Dscs[d][:, zc:zc + 1], 0.0), writes=[B_Dsc[d]])

        def chainA_late(h, d):
            oml = omlv[:, d, h:h + 1]
            S.op("dve", lambda e: e.tensor_tensor(qtTs[d], q32, RA, ALU.mult), reads=[B_q, B_R[0]], writes=[B_qts[d]])
            S.op("dve", lambda e: e.scalar_tensor_tensor(ktTs[d], RB, oml, RD, ALU.mult, ALU.mult), reads=[B_R[1], B_R[3], B_c2], writes=[B_kts[d]])

        def B_pre(h, d):
            fwd = (d == 0)
            Dt, Dsc = Dts[d], Dscs[d]
            qtT, ktT, B_qt, B_kt = qtTs[d], ktTs[d], B_qts[d], B_kts[d]
            S.op("dve", lambda e: e.tensor_copy(Dq.rearrange("p (v c) -> p v c", c=32), Dsc.unsqueeze(1).to_broadcast([128, 32, 32])),
                 reads=[B_Dsc[d]], writes=[B_Dq])
            for g in range(2):
                bk = 4 + g
                for j in range(8):
                    blk = g * 8 + j
                    S.op("pe", lambda e, blk=blk, j=j, bk=bk: e.transpose(psb16(bk)[:, j * 128:(j + 1) * 128], ktT[:, blk * 128:(blk + 1) * 128], ident),
                         reads=[B_kt, B_const], writes=[PB[bk]], inc=(j == 7))
                S.op("act", lambda e, g=g, bk=bk: e.activation(ktok[:, g * 8:(g + 1) * 8, :], psb16(bk).rearrange("p (a b) -> p a b", a=8), AF.Copy),
                     reads=[PB[bk]], writes=[B_ktok])
            msk = maskF if fwd else maskB
            for g in range(4):
                bk = 6 + (g % 2)
                for j in range(4):
                    blk = g * 4 + j
                    S.op("pe", lambda e, blk=blk, j=j, bk=bk: e.matmul(psb(bk)[:, j * 128:(j + 1) * 128], lhsT=ktT[:, blk * 128:(blk + 1) * 128],
                                                                     rhs=qtT[:, blk * 128:(blk + 1) * 128], start=True, stop=True),
                         reads=[B_kt, B_qt], writes=[PB[bk]], inc=(j == 3))
                S.op("dve", lambda e, g=g, bk=bk: e.tensor_tensor(ATm[:, g * 4:(g + 1) * 4, :].rearrange("p a b -> p (a b)"), psb(bk), msk, ALU.mult),
                     reads=[PB[bk], B_ldp], writes=[B_AT])
            Xr = Xv.rearrange("p (v c) -> p c v", c=32)
            for g in range(4):
                for half in range(2):
                    bk = 4 + half + 2 * (g % 2)
                    for j in range(4):
                        blk = g * 4 + j
                        S.op("pe", lambda e, blk=blk, half=half, j=j, bk=bk: e.matmul(psb(bk)[:, j * 128:(j + 1) * 128],
                                                                                       lhsT=ktok[half * 64:(half + 1) * 64, blk, :],
                                                                                       rhs=vtok[half * 64:(half + 1) * 64, blk, :], start=True, stop=True),
                             reads=[B_ktok, B_vtok], writes=[PB[bk]], inc=(j == 3))
                for half in range(2):
                    bk = 4 + half + 2 * (g % 2)
                    c0 = g * 8 + half
                    S.op("dve", lambda e, c0=c0, bk=bk: e.tensor_tensor(Xr[:, c0:c0 + 7:2, :], psb(bk).rearrange("p (a b) -> p a b", a=4),
                                                                       Dt[:, c0:c0 + 7:2].unsqueeze(2).to_broadcast([128, 4, 128]), ALU.mult),
                         reads=[PB[bk], B_Dt[d]], writes=[B_X])
            Sflat = Sst.rearrange("p v c -> p (v c)")
            for vq in range(4):
                sl = slice(vq * 1024, (vq + 1) * 1024)
                if fwd:
                    S.op("dve", lambda e, sl=sl: e.tensor_tensor_scan(Sflat[:, sl], Dq, Xv[:, sl], 0.0, ALU.mult, ALU.add),
                         reads=[B_X, B_Dq], writes=[B_Sst])
                else:
                    S.op("dve", lambda e, sl=sl: e.tensor_tensor_scan(Sflat[:, sl][:, ::-1], Dq[:, ::-1], Xv[:, sl][:, ::-1], 0.0, ALU.mult, ALU.add),
                         reads=[B_X, B_Dq], writes=[B_Sst])

        def OT(h, d):
            fwd = (d == 0)
            qtT, B_qt = qtTs[d], B_qts[d]
            for tb in range(4):
                bk = tb
                for j in range(4):
                    blk = tb * 4 + j
                    mm = []
                    mm.append((psb(bk)[:, j * 128:(j + 1) * 128], vtok[:, blk, :], ATm[:, blk, :], [B_vtok, B_AT]))
                    for half in range(2):
                        c = blk * 2 + half
                        cp = c - 1 if fwd else c + 1
                        if cp < 0 or cp > 31:
                            continue
                        mm.append((psb(bk)[:, j * 128 + half * 64: j * 128 + (half + 1) * 64], Sst[:, :, cp], qtT[:, c * 64:(c + 1) * 64], [B_Sst, B_qt]))
                    for i, (o_, l_, r_, rd) in enumerate(mm):
                        S.op("pe", lambda e, o_=o_, l_=l_, r_=r_, i=i, n=len(mm): e.matmul(o_, lhsT=l_, rhs=r_, start=(i == 0), stop=(i == n - 1)),
                             reads=rd, writes=[PB[bk]], inc=(j == 3 and i == len(mm) - 1))
                if fwd:
                    S.op("act", lambda e, tb=tb, bk=bk: e.activation(O32[:, tsl(tb)], psb(bk), AF.Copy), reads=[PB[bk]], writes=[B_O])
                else:
                    S.op("dve", lambda e, tb=tb, bk=bk: e.tensor_tensor(O32[:, tsl(tb)], O32[:, tsl(tb)], psb(bk), ALU.add), reads=[PB[bk], B_O], writes=[B_O])

        def outnorm(h):
            sq = ATm.rearrange("p a b -> p (a b)")
            rs = Xv[:, 0:S_LEN]
            yst = ktok.rearrange("p a b -> p (a b)")
            S.op("act", lambda e: e.activation(sq, O32, AF.Square), reads=[B_O], writes=[B_AT])
            for tb in range(4):
                bk = 4 + tb
                S.op("pe", lambda e, tb=tb, bk=bk: e.matmul(psb(bk), lhsT=ones_bf, rhs=sq[:, tsl(tb)], start=True, stop=True),
                     reads=[B_AT, B_const], writes=[PB[bk]], inc=True)
                S.op("act", lambda e, tb=tb, bk=bk: e.activation(rs[:, tsl(tb)], psb(bk), AF.Ln, bias=epsc, scale=1.0 / 128), reads=[PB[bk], B_const], writes=[B_X])
            S.op("act", lambda e: e.activation(rs, rs, AF.Exp, scale=-0.5), reads=[B_X], writes=[B_X])
            S.op("dve", lambda e: e.tensor_tensor(O32, O32, rs, ALU.mult), reads=[B_O, B_X], writes=[B_O])
            S.op("dve", lambda e: e.scalar_tensor_tensor(yst, O32, onw, sgate, ALU.mult, ALU.mult), reads=[B_O, B_sg, B_ld], writes=[B_ktok])
            S.dma("sp", ycat[h * 128:(h + 1) * 128, :], yst, reads=[B_ktok], writes=[B_ycat], slot=B_ktok)

        def hg_all():
            projQ(0); projA(0, 0); chainA_early(0, 0); chainA_late(0, 0)
            for h in range(8):
                projI(h)
                projA(h, 1)
                B_pre(h, 0)
                chainA_early(h, 1)
                chainA_late(h, 1)
                projG(h)
                OT(h, 0)
                B_pre(h, 1)
                if h + 1 < 8:
                    projQ(h + 1)
                    projA(h + 1, 0)
                OT(h, 1)
                outnorm(h)
                if h + 1 < 8:
                    chainA_early(h + 1, 0)
                    chainA_late(h + 1, 0)
                if h == 0:
                    check_stop("hg1", lambda: [dump(ktok.rearrange("p a b -> p (a b)"), 0, [B_ktok])])

        def da_heads():
            A.off = hd_off - 2 * 16 * 128
            qTa = [A.alloc([S_LEN], BF16) for _ in range(2)]; qTb = [A.alloc([S_LEN], BF16) for _ in range(2)]
            kT = [A.alloc([S_LEN], BF16) for _ in range(2)]
            vTd = A.alloc([S_LEN], BF16)
            Vx = A.alloc([16, 3, 130], BF16)
            Pt = [A.alloc([16, 1024], BF16) for _ in range(2)]
            ycTd = A.alloc([S_LEN], BF16)
            NG = 4
            o32s = [A.alloc([128], F32) for _ in range(NG)]
            t2s = [A.alloc([128], F32) for _ in range(NG)]
            ybfs = [A.alloc([128], BF16) for _ in range(NG)]
            sts = A.alloc([NG, 8], F32)
            mhalf = A.alloc([1], F32)
            B_qT = [Buf("dqT0"), Buf("dqT1")]; B_kT = [Buf("dkT0"), Buf("dkT1")]
            B_vTd, B_Vx, B_ycTd = Buf("dvT"), Buf("Vx"), Buf("dycT")
            B_P = [[Buf("P%d_%d" % (i, kb)) for kb in range(16)] for i in range(2)]
            B_g = [Buf("grp%d" % i) for i in range(NG)]
            B_mh = Buf("mhalf")
            S.op("pool", lambda e: e.memset(Vx[:, :, 0, 128:129], 1.0), writes=[B_Vx])
            for p_ in range(2):
                S.op("pool", lambda e, p_=p_: e.memset(qTa[p_][64:128, :], 0.0), writes=[B_qT[p_]])
                S.op("pool", lambda e, p_=p_: e.memset(qTb[p_][0:64, :], 0.0), writes=[B_qT[p_]])
            S.op("pool", lambda e: e.memset(mhalf, -0.5), writes=[B_mh])
            if stop is not None:
                load_slab(0, 5120); load_slab(1, 6144); load_slab(2, 7168)
            cnt = [0]
            gcnt = [0]
            DA_COL = [5120, 6144, 7168]

            def proj_units(h):
                p_ = h % 2
                lst = []
                for si in range(3):
                    for tb in range(4):
                        def u(si=si, tb=tb):
                            pb_ = 4 + (si * 4 + tb) % 4
                            for fc in range(16):
                                S.op("pe", lambda e, fc=fc: e.matmul(psb(pb_), lhsT=slabs[si][:, fc, :], rhs=uT[:, fc, tb * 512:(tb + 1) * 512],
                                                                     start=(fc == 0), stop=(fc == 15)),
                                     reads=[B_slab[si]] + B_uT, writes=[PB[pb_]], inc=(fc == 15))
                            if si == 0:
                                S.op("dve", lambda e: e.tensor_scalar(qTa[p_][0:64, tsl(tb)], psb(pb_)[0:64, :], 0.125, None, ALU.mult), reads=[PB[pb_]], writes=[B_qT[p_]])
                                S.op("dve", lambda e: e.tensor_scalar(qTb[p_][64:128, tsl(tb)], psb(pb_)[64:128, :], 0.125, None, ALU.mult), reads=[PB[pb_]], writes=[B_qT[p_]])
                            elif si == 1:
                                S.op("dve", lambda e: e.tensor_copy(kT[p_][:, tsl(tb)], psb(pb_)), reads=[PB[pb_]], writes=[B_kT[p_]])
                            else:
                                S.op("dve", lambda e: e.tensor_copy(vTd[:, tsl(tb)], psb(pb_)), reads=[PB[pb_]], writes=[B_vTd])
                            if tb == 3:
                                if h + 1 < 8:
                                    load_slab(si, DA_COL[si] + (h + 1) * 128)
                                elif si == 2 and stop is None:
                                    p3_early_loads(False)
                        lst.append(u)
                return lst

            for u in proj_units(0):
                u()
            for h in range(8):
                p_ = h % 2
                if h == 7 and stop is None:
                    p3_early_ycs()
                eA = efar[:, 2 * h:2 * h + 1]
                eB = efar[:, 2 * h + 1:2 * h + 2]
                for g in range(2):
                    bk = 6 + g
                    for j in range(8):
                        blk = g * 8 + j
                        S.op("pe", lambda e, blk=blk, j=j, bk=bk: e.transpose(psb16(bk)[:, j * 128:(j + 1) * 128], vTd[:, blk * 128:(blk + 1) * 128], ident),
                             reads=[B_vTd, B_const], writes=[PB[bk]], inc=(j == 7))
                    src = psb16(bk).rearrange("p (a b) -> p a b", a=8)
                    S.op("dve", lambda e, g=g, src=src: e.tensor_copy(Vx[:, g * 8:(g + 1) * 8, 0, 0:128], src), reads=[PB[bk]], writes=[B_Vx])
                    S.op("act", lambda e, g=g, src=src: e.activation(Vx[:, g * 8:(g + 1) * 8, 1, 0:128], src, AF.Copy, scale=eA), reads=[PB[bk], B_c2], writes=[B_Vx])
                    S.op("act", lambda e, g=g, src=src: e.activation(Vx[:, g * 8:(g + 1) * 8, 2, 0:128], src, AF.Copy, scale=eB), reads=[PB[bk], B_c2], writes=[B_Vx])
                S.op("dve", lambda e: e.tensor_copy(Vx[:, :, 1, 128:129], eA.unsqueeze(1).to_broadcast([128, 16, 1])), reads=[B_c2], writes=[B_Vx])
                S.op("dve", lambda e: e.tensor_copy(Vx[:, :, 2, 128:129], eB.unsqueeze(1).to_broadcast([128, 16, 1])), reads=[B_c2], writes=[B_Vx])

                def qk(qc, kb, pi):
                    sp_ = (cnt[0] % 2) * 2
                    cnt[0] += 1
                    for m, qsrc in enumerate((qTa[p_], qTb[p_])):
                        S.op("pe", lambda e, m=m, qsrc=qsrc, sp_=sp_: e.matmul(psb(sp_ + m), lhsT=kT[p_][:, kb * 128:(kb + 1) * 128],
                                                                                rhs=qsrc[:, qc * 512:(qc + 1) * 512], start=True, stop=True),
                             reads=[B_kT[p_], B_qT[p_]], writes=[PB[sp_ + m]], inc=True)
                    S.op("act", lambda e, sp_=sp_: e.activation(Pt[pi][:, kb, :], psb(sp_, 2), AF.Exp), reads=[PB[sp_], PB[sp_ + 1]], writes=[B_P[pi][kb]])
                    lo = max(kb - 1, 4 * qc)
                    hi = min(kb + 1, 4 * qc + 3)
                    if lo <= hi:
                        n = (hi - lo + 1) * 128
                        c0 = (lo - 4 * qc) * 128
                        e0 = (lo - (kb - 1)) * 128
                        pv_ = Pt[pi][:, kb, :].rearrange("p (m q) -> p m q", m=2)[:, :, c0:c0 + n]
                        ev_ = Eall[:, h, e0:e0 + n].unsqueeze(1).to_broadcast([128, 2, n])
                        S.op("dve", lambda e, pv_=pv_, ev_=ev_: e.tensor_tensor(pv_, pv_, ev_, ALU.mult), reads=[B_c2], writes=[B_P[pi][kb]])

                fin_q = []

                def pv_steps(qc, qb, pi):
                    qbg = 4 * qc + qb
                    gi = gcnt[0] % NG
                    gcnt[0] += 1
                    accs = []
                    mms = []
                    for m in range(2):
                        a = (qb % 2) * 2 + m
                        bk = 4 + a // 2
                        col = (a % 2) * 130
                        acc = psb(bk)[:, col:col + 129]
                        accs.append((acc, bk))
                        for kb in range(16):
                            var = 1 if kb > qbg + 1 else (2 if kb < qbg - 1 else 0)
                            mms.append((acc, bk, kb, var, m))

                    def emit_mm(lst):
                        for (acc, bk, kb, var, m) in lst:
                            S.op("pe", lambda e, acc=acc, kb=kb, var=var, m=m: e.matmul(acc, lhsT=Pt[pi][:, kb, m * 512 + qb * 128: m * 512 + (qb + 1) * 128],
                                                                                          rhs=Vx[:, kb, var, 0:129], start=(kb == 0), stop=(kb == 15)),
                                 reads=[B_P[pi][kb], B_Vx], writes=[PB[bk]], inc=(kb == 15))

                    def chain():
                        (a1, b1), (a2, b2) = accs
                        stq = sts[:, gi, :]
                        o32, t2, ybf, Bg = o32s[gi], t2s[gi], ybfs[gi], B_g[gi]
                        S.op("dve", lambda e: e.reciprocal(stq[:, 0:1], a1[:, 128:129]), reads=[PB[b1]], writes=[Bg])
                        S.op("dve", lambda e: e.reciprocal(stq[:, 1:2], a2[:, 128:129]), reads=[PB[b2]], writes=[Bg])
                        S.op("dve", lambda e: e.tensor_tensor(stq[:, 2:3], stq[:, 1:2], neglam, ALU.mult), reads=[Bg, B_c2], writes=[Bg])
                        S.op("dve", lambda e: e.tensor_scalar(t2, a2[:, 0:128], stq[:, 2:3], None, ALU.mult), reads=[PB[b2], Bg], writes=[Bg])
                        S.op("dve", lambda e: e.scalar_tensor_tensor(o32, a1[:, 0:128], stq[:, 0:1], t2, ALU.mult, ALU.add), reads=[PB[b1], Bg], writes=[Bg])
                        S.op("dve", lambda e: e.scalar_tensor_tensor(t2, o32, 1.0, o32, ALU.mult, ALU.mult, accum_out=stq[:, 3:4]), reads=[Bg], writes=[Bg])
                        S.op("pool", lambda e: e.tensor_scalar(stq[:, 4:5], stq[:, 3:4], 1.0 / 128, EPS, ALU.mult, ALU.add), reads=[Bg], writes=[Bg])
                        S.op("pool", lambda e: e.tensor_tensor(stq[:, 5:6], stq[:, 4:5], mhalf, ALU.pow), reads=[Bg, B_mh], writes=[Bg])
                        S.op("dve", lambda e: e.scalar_tensor_tensor(ybf, o32, stq[:, 5:6], sublnbc, ALU.mult, ALU.mult), reads=[Bg, B_c2], writes=[Bg])

                        def fin():
                            tv = psb16(6)[:, (gi % 2) * 128:(gi % 2 + 1) * 128]
                            S.op("pe", lambda e: e.transpose(tv, ybf, ident), reads=[Bg, B_const], writes=[PB[6]], inc=True)
                            S.op("dve", lambda e: e.tensor_copy(ycTd[:, qbg * 128:(qbg + 1) * 128], tv), reads=[PB[6]], writes=[B_ycTd])
                        fin_q.append(fin)

                    steps = []
                    for i in range(4):
                        part = mms[i * 8:(i + 1) * 8]
                        if i < 3:
                            steps.append(lambda part=part: emit_mm(part))
                        else:
                            def last(part=part):
                                emit_mm(part)
                                while len(fin_q) > 0:
                                    fin_q.pop(0)()
                                chain()
                            steps.append(last)
                    return steps

                nxt = proj_units(h + 1) if h + 1 < 8 else []
                for qc in range(5):
                    steps = []
                    if qc > 0:
                        for qb in range(4):
                            steps += pv_steps(qc - 1, qb, (qc - 1) % 2)
                    for kb in range(16):
                        if qc < 4:
                            qk(qc, kb, qc % 2)
                        if qc == 0 and len(nxt) > 0:
                            nxt.pop(0)()
                        if qc > 0:
                            steps[kb]()
                while len(nxt) > 0:
                    nxt.pop(0)()
                while len(fin_q) > 0:
                    fin_q.pop(0)()
                S.dma("sp", ycat[1024 + h * 128:1024 + (h + 1) * 128, :], ycTd, reads=[B_ycTd], writes=[B_ycat], slot=B_ycTd)
                if h == 0:
                    check_stop("da1", lambda: [dump(ycTd, 0, [B_ycTd]), dump(kT[0], 2 * S_LEN, [B_kT[0]]), dump(vTd, 3 * S_LEN, [B_vTd])])

        if stop != "da1":
            hg_all()
        if stop is None:
            emit_conv(1000)
        S.barrier()
        _save = A.off
        A.off = base_off
        w2bc = A.alloc([DM], F32); wfbc = A.alloc([DM], F32); B_w23 = Buf("w23")
        hx = A.alloc([4, DM], F32); B_hx = [Buf("hx%d" % i) for i in range(4)]
        ycs = A.alloc([16, 512], BF16); B_ycs = Buf("ycs")
        p3_rest_off = A.off
        assert p3_rest_off <= base_off + 16 * S_LEN, "phase-3 early buffers must fit inside uT"
        A.off = _save

        def load_x(ch, tt, extra=()):
            t0_ = ch * 512 + tt * 128
            S.dma("pool", hx[:, tt, :], x[t0_:t0_ + 128, :], writes=[B_hx[tt]] + list(extra))

        def p3_early_loads(with_ycs=True):
            S.dma("pool", w2bc, norm2_w.partition_broadcast(128), writes=[B_w23] + B_uT)
            S.dma("pool", wfbc, final_w.partition_broadcast(128), writes=[B_w23] + B_uT)
            for tt in range(4):
                load_x(0, tt, extra=B_uT)
            if with_ycs:
                p3_early_ycs()

        def p3_early_ycs():
            S.dma("pool", ycs[:, 0:15, :], ycat[0:1920, 0:512].rearrange("(c p) t -> p c t", p=128), reads=[B_ycat], writes=[B_ycs] + B_uT)

        da_heads()
        S.barrier()

        A.off = p3_rest_off
        u2T = A.alloc([16, 512], BF16); B_u2T = [Buf("u2Ta"), Buf("u2Tb")]
        hid = A.alloc([NJ, 512], BF16); B_hid = Buf("hid")
        NU = 5
        units = [A.alloc([4096], BF16) for _ in range(NU)]; B_un = [Buf("unit%d" % i) for i in range(NU)]
        ub3 = [A.alloc([DM], BF16) for _ in range(2)]; B_ub3 = [Buf("ub3_%d" % i) for i in range(2)]
        sgs = [A.alloc([512], F32) for _ in range(2)]; B_sgs = [Buf("sgs%d" % i) for i in range(2)]
        ob = [A.alloc([DM], F32) for _ in range(2)]; B_ob = [Buf("ob%d" % i) for i in range(2)]
        st3 = A.alloc([64, 4], F32)
        B_out = Buf("out_hbm")
        urot = [0]

        def load_unit(src_ap, nel):
            i = urot[0] % NU
            urot[0] += 1
            S.dma("sp", units[i][:, 0:nel], src_ap, writes=[B_un[i]])
            return units[i], B_un[i]

        def rms3(src, B_src, wbc, dst, B_dst, stc):
            rms_token_tile(src, B_src, wbc, B_w23, dst, B_dst, stc, Buf("st3"))

        def load_ycs(ch):
            S.dma("pool", ycs, ycat[:, ch * 512:(ch + 1) * 512].rearrange("(c p) t -> p c t", p=128), reads=[B_ycat], writes=[B_ycs])

        S.dma("pool", ycs[:, 15, :], ycat[1920:2048, 0:512], reads=[B_ycat], writes=[B_ycs])
        for ch in range(4):
            t0 = ch * 512
            for cb in range(4):
                bks = [(4 if cb % 2 == 0 else 0) + tt for tt in range(4)]
                for hf in range(2):
                    un, Bu = load_unit(wo_s[cb, hf].rearrange("p fc n -> p (fc n)"), 4096)
                    uv = un.rearrange("p (fc n) -> p fc n", fc=8)
                    for tt in range(4):
                        for fc in range(8):
                            S.op("pe", lambda e, fc=fc, tt=tt, hf=hf, uv=uv: e.matmul(psb(bks[tt]), lhsT=ycs[:, hf * 8 + fc, tt * 128:(tt + 1) * 128], rhs=uv[:, fc, :],
                                                                                       start=(hf == 0 and fc == 0), stop=(hf == 1 and fc == 7)),
                                 reads=[B_ycs, Bu], writes=[PB[bks[tt]]], inc=(fc == 7))
                for tt in range(4):
                    hv = hx[:, tt, cb * 512:(cb + 1) * 512]
                    S.op("dve", lambda e, hv=hv, bk=bks[tt]: e.tensor_tensor(hv, hv, psb(bk), ALU.add), reads=[PB[bks[tt]], B_hx[tt]], writes=[B_hx[tt]])
            if ch + 1 < 4:
                load_ycs(ch + 1)
            for tt in range(5):
                if tt < 4:
                    rms3(hx[:, tt, :], B_hx[tt], w2bc, ub3[tt % 2], B_ub3[tt % 2], st3[:, ch * 8 + tt, :])
                if tt >= 1:
                    t_ = tt - 1
                    transpose_tile_to(ub3[t_ % 2], B_ub3[t_ % 2], u2T, B_u2T, t_ * 128, (0, 1) if t_ % 2 == 0 else (2, 3), ("act", "dve"))
            for pr in range(NJ // 2):
                base = 4 if pr % 2 == 0 else 0
                for hf in range(2):
                    un, Bu = load_unit(wgu_s[pr, hf].rearrange("p fc g n -> p (fc g n)"), 4096)
                    uv = un.rearrange("p (fc g n) -> p fc g n", fc=8, g=2)
                    for jj in range(2):
                        for g in range(2):
                            bk = base + jj * 2 + g
                            for fc in range(8):
                                S.op("pe", lambda e, fc=fc, g=g, jj=jj, bk=bk, hf=hf, uv=uv: e.matmul(psb(bk), lhsT=uv[:, fc, g, jj * 128:(jj + 1) * 128],
                                                                                                       rhs=u2T[:, hf * 8 + fc, :],
                                                                                                       start=(hf == 0 and fc == 0), stop=(hf == 1 and fc == 7)),
                                     reads=[Bu] + B_u2T, writes=[PB[bk]], inc=(fc == 7))
                for jj in range(2):
                    j = pr * 2 + jj
                    k = j % 2
                    bg, bu = base + jj * 2, base + jj * 2 + 1
                    S.op("act", lambda e, k=k, bg=bg: e.activation(sgs[k], psb(bg), AF.Silu), reads=[PB[bg]], writes=[B_sgs[k]])
                    S.op("dve", lambda e, k=k, bu=bu, j=j: e.tensor_tensor(hid[:, j, :], sgs[k], psb(bu), ALU.mult), reads=[B_sgs[k], PB[bu]], writes=[B_hid])
            for cb in range(4):
                bks = [(4 if cb % 2 == 0 else 0) + tt for tt in range(4)]
                for (j0, nj) in ((0, 8), (8, 8), (16, 8), (24, 8), (32, 8), (40, 4)):
                    un, Bu = load_unit(wd_s[cb, :, j0:j0 + nj, :].rearrange("p j n -> p (j n)"), nj * 512)
                    uv = un[:, 0:nj * 512].rearrange("p (j n) -> p j n", j=nj)
                    for tt in range(4):
                        for jl in range(nj):
                            j = j0 + jl
                            S.op("pe", lambda e, j=j, jl=jl, tt=tt, uv=uv: e.matmul(psb(bks[tt]), lhsT=hid[:, j, tt * 128:(tt + 1) * 128], rhs=uv[:, jl, :],
                                                                                     start=(j == 0), stop=(j == NJ - 1)),
                                 reads=[B_hid, Bu], writes=[PB[bks[tt]]], inc=(jl == nj - 1))
                for tt in range(4):
                    hv = hx[:, tt, cb * 512:(cb + 1) * 512]
                    S.op("dve", lambda e, hv=hv, bk=bks[tt]: e.tensor_tensor(hv, hv, psb(bk), ALU.add), reads=[PB[bks[tt]], B_hx[tt]], writes=[B_hx[tt]])
            for tt in range(4):
                s_ = tt % 2
                rms3(hx[:, tt, :], B_hx[tt], wfbc, ob[s_], B_ob[s_], st3[:, ch * 8 + 4 + tt, :])
                S.dma("pool", out[t0 + tt * 128:t0 + (tt + 1) * 128, :], ob[s_], reads=[B_ob[s_]], writes=[B_out], slot=B_ob[s_])
                if ch + 1 < 4:
                    load_x(ch + 1, tt)

        S.barrier()
        S.emit()
    return nc


def _rel_bucket_table():
    N_BUCKETS, MAX_DISTANCE = 32, 128
    nb = N_BUCKETS // 2
    max_exact = nb // 2
    try:
        import jax
        import jax.numpy as jnp

        def rel_bucket(rel):
            ret = jnp.where(rel > 0, nb, 0)
            n = jnp.abs(rel)
            nf = jnp.maximum(n, 1).astype(jnp.float32)
            large = max_exact + (jnp.log(nf / max_exact) / math.log(MAX_DISTANCE / max_exact)
                                 * (nb - max_exact)).astype(jnp.int32)
            large = jnp.minimum(large, nb - 1)
            return ret + jnp.where(n < max_exact, n, large)

        with jax.default_device(jax.devices("cpu")[0]):
            return np.asarray(rel_bucket(jnp.arange(-255, 256, dtype=jnp.int32)))
    except Exception:
        rel = np.arange(-255, 256, dtype=np.int32)
        ret = np.where(rel > 0, nb, 0)
        n = np.abs(rel)
        nf = np.maximum(n, 1).astype(np.float32)
        large = max_exact + (np.log(nf / np.float32(max_exact)) / np.float32(math.log(MAX_DISTANCE / max_exact))
                             * np.float32(nb - max_exact)).astype(np.int32)
        large = np.minimum(large, nb - 1)
        return ret + np.where(n < max_exact, n, large)


_NC_CACHE = {}


def kernel(x, norm1_w, w_in, hg_lb_logits, hg_onorm_w, lambda_q1, lambda_k1, lambda_q2, lambda_k2,
           da_subln_w, rel_bias, w_out, norm2_w, w_gate, w_up, w_down, final_norm_w):
    f32 = lambda a: np.ascontiguousarray(np.asarray(a, dtype=np.float32))
    x = f32(x)
    rel_bias = f32(rel_bias)
    bucket = _rel_bucket_table()
    kk = np.arange(128)[:, None]
    qq = np.arange(384)[None, :]
    rel = kk - (qq - 128)
    bidx = bucket[rel + 255]
    bias_near = np.ascontiguousarray(np.transpose(rel_bias[bidx], (0, 2, 1))).reshape(128, 8 * 384)
    bias_far = np.ascontiguousarray(np.stack([rel_bias[31, :], rel_bias[15, :]], axis=1)).reshape(16)
    s_ = np.arange(128)[:, None]
    t_ = np.arange(128)[None, :]
    same = (s_ // 64) == (t_ // 64)
    mF = (same & (s_ <= t_)).astype(np.float32)
    mB = (same & (s_ >= t_)).astype(np.float32)
    hmask = np.ascontiguousarray(np.concatenate([np.tile(mF, (1, 4)), np.tile(mB, (1, 4))], axis=1))
    lam_in = np.ascontiguousarray(np.stack([f32(lambda_q1)[0], f32(lambda_k1)[0], f32(lambda_q2)[0], f32(lambda_k2)[0]], axis=0))
    shared = {
        "w_in": f32(w_in)[0], "w_out": f32(w_out)[0], "w_gate": f32(w_gate)[0], "w_up": f32(w_up)[0], "w_down": f32(w_down)[0],
        "norm1_w": f32(norm1_w)[0], "norm2_w": f32(norm2_w)[0], "final_w": f32(final_norm_w),
        "lb_logits": f32(hg_lb_logits), "onorm_w": f32(hg_onorm_w)[0], "subln_w": f32(da_subln_w)[0],
        "lam_in": lam_in, "bias_near": bias_near, "bias_far": bias_far, "hmask": hmask,
    }
    if "nc" not in _NC_CACHE:
        _NC_CACHE["nc"] = build_nc()
    nc = _NC_CACHE["nc"]
    n = x.shape[0]
    in_maps = [dict(shared, x=x[b]) for b in range(n)]
    res = run_bass_kernel_spmd(nc, in_maps, core_ids=list(range(n)))
    return np.stack([np.asarray(r["out"], dtype=np.float32) for r in res.results], axis=0)
```

```python
import math
from contextlib import ExitStack

import numpy as np
import concourse.bass as bass
import concourse.mybir as mybir
from concourse.bass_utils import run_bass_kernel_spmd

F32 = mybir.dt.float32
BF16 = mybir.dt.bfloat16
AF = mybir.ActivationFunctionType
ALU = mybir.AluOpType

ENGS = ("pe", "act", "dve", "pool", "sp")
S_LEN = 2048
DM = 2048
DFF = 5632
NJ = DFF // 128
EPS = 1e-6
LAM_INIT = 0.8 - 0.6 * math.exp(-0.3 * 0)


class Buf:
    def __init__(self, name):
        self.name = name
        self.w = None
        self.r = {}
        self.dsem = None
        self.dcnt = 0


class _Rec:
    def __init__(self):
        self.calls = []

    def __getattr__(self, name):
        def f(*a, **k):
            self.calls.append((name, a, k))
            return self
        return f


class Sched:
    def __init__(self, nc, es):
        self.nc = nc
        self.es = es
        self.prog = {e: [] for e in ENGS}
        self.sem = {}
        self.cnt = {}
        for e in ("pe", "act", "dve", "pool"):
            self.sem[e] = es.enter_context(nc.semaphore("sem_" + e))
            self.cnt[e] = 0
        self.known = {e: {} for e in ENGS}
        self.pe_pending = False
        self.dbufs = []

    def _tokens(self, reads, writes):
        toks = []
        for b in reads:
            if b.w is not None:
                toks.append(b.w)
        for b in writes:
            if b.w is not None:
                toks.append(b.w)
            toks.extend(b.r.values())
        return toks

    def _wait(self, e, toks):
        kn = self.known[e]
        need = {}
        for (sem, v) in toks:
            if e == "pe" and sem is self.sem["pe"]:
                continue
            k = id(sem)
            if kn.get(k, 0) < v and need.get(k, (None, 0))[1] < v:
                need[k] = (sem, v)
        for k, (sem, v) in need.items():
            kn[k] = v
            self.prog[e].append(lambda eng, sem=sem, v=v: eng.wait_ge(sem, v))

    def _record(self, tok, reads, writes):
        k = id(tok[0])
        for b in reads:
            if b.r.get(k, (None, 0))[1] < tok[1]:
                b.r[k] = tok
        for b in writes:
            b.w = tok
            b.r = {}

    def op(self, e, fn, reads=(), writes=(), inc=True):
        rec = _Rec()
        fn(rec)
        assert len(rec.calls) == 1
        name, a, k = rec.calls[0]
        fn = lambda eng, name=name, a=a, k=k: getattr(eng, name)(*a, **k)
        self._wait(e, self._tokens(reads, writes))
        sem = self.sem[e]
        if inc:
            self.cnt[e] += 1
            tok = (sem, self.cnt[e])
            self.prog[e].append(lambda eng, fn=fn, sem=sem: fn(eng).then_inc(sem, 1))
            if e == "pe":
                self.pe_pending = False
        else:
            assert e == "pe"
            tok = (sem, self.cnt[e] + 1)
            self.prog[e].append(lambda eng, fn=fn: fn(eng))
            self.pe_pending = True
        self._record(tok, reads, writes)

    def dma(self, q, out_ap, in_ap, reads=(), writes=(), slot=None, concurrent=False, **kw):
        if slot is None:
            slot = writes[0] if writes else reads[0]
        if slot.dsem is None:
            slot.dsem = self.es.enter_context(self.nc.semaphore("dsem_" + slot.name))
            self.dbufs.append(slot)
        toks = self._tokens(reads, writes)
        if concurrent:
            toks = [t for t in toks if t[0] is not slot.dsem]
        self._wait(q, toks)
        slot.dcnt += 16
        tok = (slot.dsem, slot.dcnt)
        self.prog[q].append(
            lambda eng, o=out_ap, i=in_ap, s=slot.dsem, kw=kw: eng.dma_start(out=o, in_=i, **kw).then_inc(s, 16))
        self._record(tok, reads, writes)

    def barrier(self):
        assert not self.pe_pending
        toks = [(self.sem[e], self.cnt[e]) for e in ("pe", "act", "dve", "pool") if self.cnt[e] > 0]
        toks += [(b.dsem, b.dcnt) for b in self.dbufs if b.dcnt > 0]
        for e in ENGS:
            self._wait(e, toks)

    def emit(self):
        assert not self.pe_pending
        nc = self.nc
        engmap = {"pe": "tensor", "act": "scalar", "dve": "vector", "pool": "gpsimd", "sp": "sync"}
        with nc.Block() as block:
            for e in ENGS:
                prog = self.prog[e]

                def body(eng, prog=prog):
                    for f in prog:
                        f(eng)
                getattr(block, engmap[e])(body)


class Arena:
    def __init__(self, ap_bf16):
        self.t = ap_bf16
        self.off = 0
        self.cap = ap_bf16.shape[1]

    def alloc(self, free_shape, dt):
        n = 1
        for s in free_shape:
            n *= s
        nel = n * 2 if dt == F32 else n
        nel = (nel + 15) // 16 * 16
        assert self.off + nel <= self.cap, ("arena overflow", self.off, nel, self.cap)
        v = self.t[:, self.off:self.off + (n * 2 if dt == F32 else n)]
        self.off += nel
        if dt == F32:
            v = v.bitcast(F32)
        if len(free_shape) == 2:
            v = v.rearrange("p (a b) -> p a b", a=free_shape[0])
        elif len(free_shape) == 3:
            v = v.rearrange("p (a b c) -> p a b c", a=free_shape[0], b=free_shape[1])
        return v


class _Stop(Exception):
    pass


def build_nc(stop=None):
    holder = {}
    try:
        _build(holder, stop)
    except _Stop:
        pass
    return holder["nc"]


def _build(holder, stop):
    nc = bass.Bass("TRN2", target_bir_lowering=False)
    holder["nc"] = nc
    dt_in = lambda name, shape: nc.dram_tensor(name, shape, F32, kind="ExternalInput").ap()
    x = dt_in("x", [S_LEN, DM])
    w_in = dt_in("w_in", [DM, 8192])
    w_out = dt_in("w_out", [DM, DM])
    w_gate = dt_in("w_gate", [DM, DFF])
    w_up = dt_in("w_up", [DM, DFF])
    w_down = dt_in("w_down", [DFF, DM])
    norm1_w = dt_in("norm1_w", [DM])
    norm2_w = dt_in("norm2_w", [DM])
    final_w = dt_in("final_w", [DM])
    lb_logits = dt_in("lb_logits", [2, 2, 1024])
    onorm_w = dt_in("onorm_w", [128])
    subln_w = dt_in("subln_w", [128])
    lam_in = dt_in("lam_in", [4, 64])
    bias_near = dt_in("bias_near", [128, 8 * 384])
    bias_far = dt_in("bias_far", [16])
    hmask = dt_in("hmask", [128, 2 * 512])
    out = nc.dram_tensor("out", [S_LEN, DM], F32, kind="ExternalOutput").ap()
    ycat = nc.dram_tensor("ycat", [DM, S_LEN], BF16, kind="Internal").ap()
    wo_s = nc.dram_tensor("wo_s", [4, 2, 128, 8, 512], BF16, kind="Internal").ap()
    wgu_s = nc.dram_tensor("wgu_s", [22, 2, 128, 8, 2, 256], BF16, kind="Internal").ap()
    wd_s = nc.dram_tensor("wd_s", [4, 128, NJ, 512], BF16, kind="Internal").ap()
    if stop is not None:
        dbg_bf = nc.dram_tensor("dbg_bf", [128, 16 * S_LEN], BF16, kind="ExternalOutput").ap()
        dbg_f = nc.dram_tensor("dbg_f", [128, 8 * S_LEN], F32, kind="ExternalOutput").ap()

    with ExitStack() as es:
        S = Sched(nc, es)
        arena_t = es.enter_context(nc.sbuf_tensor("arena", [128, 103 * 1024], BF16))
        A = Arena(arena_t[:])
        ps = es.enter_context(nc.psum_tensor("ps", [128, 4096], F32))
        PB = [Buf("psb%d" % i) for i in range(8)]

        def psb(i, n=1):
            return ps[:, i * 512:(i + n) * 512]

        def psb16(i):
            return ps[:, i * 512:(i + 1) * 512].bitcast(BF16)

        ident = A.alloc([128], BF16); B_const = Buf("const")
        ones_bf = A.alloc([128], BF16)
        maskF = A.alloc([512], BF16)
        maskB = A.alloc([512], BF16)
        lbt = A.alloc([2, 2, 8], F32)
        lbv = A.alloc([2, 8], F32)
        omlv = A.alloc([2, 8], F32)
        onw = A.alloc([1], F32)
        sublnbc = A.alloc([128], F32)
        lamt = A.alloc([4, 64], F32)
        lams = A.alloc([8], F32)
        lamj = A.alloc([64], F32)
        efar = A.alloc([16], F32)
        mhalfc = A.alloc([1], F32)
        epsc = A.alloc([1], F32)
        Eall = A.alloc([8, 384], BF16)
        Etmp = None

        S.op("pool", lambda e: e.memset(ones_bf, 1.0), writes=[B_const])
        S.op("pool", lambda e: e.memset(mhalfc, -0.5), writes=[B_const])
        S.op("pool", lambda e: e.memset(epsc, EPS), writes=[B_const])
        S.op("pool", lambda e: e.affine_select(ident, ones_bf, [[-1, 128]], ALU.is_equal, 0.0, base=0, channel_multiplier=1),
             writes=[B_const])
        B_ld = Buf("cld")
        B_ldp = Buf("cldp")
        S.dma("pool", maskF, hmask[:, 0:512], writes=[B_ldp])
        S.dma("pool", maskB, hmask[:, 512:1024], writes=[B_ldp])
        for d in range(2):
            for sl in range(2):
                S.dma("sp", lbt[:, d, sl, :], lb_logits[d, sl].rearrange("(h p) -> p h", p=128), writes=[B_ld],
                      allow_slow_non_contiguous=True, concurrent=True)
        S.dma("sp", onw, onorm_w.rearrange("(p o) -> p o", o=1), writes=[B_ld], allow_slow_non_contiguous=True, concurrent=True)
        S.dma("sp", sublnbc, subln_w.partition_broadcast(128), writes=[B_ld], concurrent=True)
        S.dma("sp", lamt.rearrange("p a b -> p (a b)"), lam_in.rearrange("a b -> (a b)").partition_broadcast(128), writes=[B_ld], concurrent=True)
        S.dma("sp", efar, bias_far.partition_broadcast(128), writes=[B_ld], concurrent=True)
        B_c2 = Buf("const2")
        S.op("dve", lambda e: e.tensor_tensor(lbv, lbt[:, :, 0, :], lbt[:, :, 1, :], ALU.subtract), reads=[B_ld], writes=[B_c2])
        S.op("act", lambda e: e.activation(lbv, lbv, AF.Sigmoid), reads=[B_c2], writes=[B_c2])
        S.op("dve", lambda e: e.tensor_scalar(omlv, lbv, -1.0, 1.0, ALU.mult, ALU.add), reads=[B_c2], writes=[B_c2])
        S.op("dve", lambda e: e.tensor_scalar(sublnbc, sublnbc, 1.0 - LAM_INIT, None, ALU.mult), reads=[B_ld], writes=[B_c2])
        S.op("dve", lambda e: e.tensor_tensor(lamj, lamt[:, 0, :], lamt[:, 1, :], ALU.mult), reads=[B_ld], writes=[B_c2])
        S.op("dve", lambda e: e.tensor_reduce(lams[:, 0:1], lamj, mybir.AxisListType.X, ALU.add), reads=[B_c2], writes=[B_c2])
        S.op("dve", lambda e: e.tensor_tensor(lamj, lamt[:, 2, :], lamt[:, 3, :], ALU.mult), reads=[B_ld, B_c2], writes=[B_c2])
        S.op("dve", lambda e: e.tensor_reduce(lams[:, 1:2], lamj, mybir.AxisListType.X, ALU.add), reads=[B_c2], writes=[B_c2])
        S.op("act", lambda e: e.activation(lams[:, 2:4], lams[:, 0:2], AF.Exp), reads=[B_c2], writes=[B_c2])
        S.op("dve", lambda e: e.tensor_tensor(lams[:, 4:5], lams[:, 2:3], lams[:, 3:4], ALU.subtract), reads=[B_c2], writes=[B_c2])
        S.op("dve", lambda e: e.tensor_scalar(lams[:, 5:6], lams[:, 4:5], -1.0, -LAM_INIT, ALU.mult, ALU.add), reads=[B_c2], writes=[B_c2])
        neglam = lams[:, 5:6]
        S.op("act", lambda e: e.activation(efar, efar, AF.Exp), reads=[B_ld], writes=[B_c2])
        CONST = [B_const, B_c2, B_ld, B_ldp]
        base_off = A.off

        B_dbg = Buf("dbg")

        def dump(ap, col0, rd):
            tgt = dbg_f if ap.dtype == F32 else dbg_bf
            S.dma("sp", tgt[:, col0:col0 + ap.shape[1]], ap, reads=rd, writes=[B_dbg], slot=B_dbg)

        def check_stop(tag, fn=None):
            if stop == tag:
                if fn is not None:
                    fn()
                S.barrier()
                S.emit()
                raise _Stop()

        uT = A.alloc([16, S_LEN], BF16); B_uT = [Buf("uTa"), Buf("uTb")]
        NSLAB = 5
        slabs = [A.alloc([16, 128], BF16) for _ in range(NSLAB)]; B_slab = [Buf("slab%d" % i) for i in range(NSLAB)]
        hd_off = A.off
        mskf = A.alloc([S_LEN], BF16); B_msk = Buf("msk")
        S.op("pool", lambda e: e.memset(mskf, 1.0), writes=[B_msk])
        S.op("pool", lambda e: e.memset(mskf[:, 0::64], 0.0), writes=[B_msk])

        def load_slab(i, col0):
            S.dma("pool", slabs[i], w_in[:, col0:col0 + 128].rearrange("(c p) n -> p c n", p=128), writes=[B_slab[i]])

        p2_off = A.off
        w1bc = A.alloc([DM], F32); B_w1 = Buf("w1bc")
        NX = 4
        xts = [A.alloc([DM], F32) for _ in range(NX)]; B_xt = [Buf("xt%d" % i) for i in range(NX)]
        ubs = [A.alloc([DM], BF16) for _ in range(3)]; B_ub = [Buf("ub%d" % i) for i in range(3)]
        st1 = A.alloc([16, 4], F32)
        Etmp = A.alloc([8 * 384], F32); B_Et = Buf("Etmp")
        S.dma("sp", Etmp, bias_near, writes=[B_Et])
        S.op("act", lambda e: e.activation(Eall.rearrange("p a b -> p (a b)"), Etmp, AF.Exp), reads=[B_Et], writes=[B_c2])
        S.dma("sp", w1bc, norm1_w.partition_broadcast(128), writes=[B_w1])

        def rms_token_tile(src, B_src, wbc, B_wbc, dst, B_dst, stc, B_st):
            jv = dst if dst.dtype == BF16 else dst.bitcast(BF16)[:, 0:DM]
            S.op("act", lambda e: e.activation(jv, src, AF.Square, accum_out=stc[:, 0:1]), reads=[B_src], writes=[B_dst, B_st])
            S.op("pool", lambda e: e.tensor_scalar(stc[:, 1:2], stc[:, 0:1], 1.0 / DM, EPS, ALU.mult, ALU.add), reads=[B_st], writes=[B_st])
            S.op("pool", lambda e: e.tensor_tensor(stc[:, 2:3], stc[:, 1:2], mhalfc, ALU.pow), reads=[B_st, B_const], writes=[B_st])
            S.op("dve", lambda e: e.scalar_tensor_tensor(dst, src, stc[:, 2:3], wbc, ALU.mult, ALU.mult),
                 reads=[B_src, B_st, B_wbc], writes=[B_dst])

        def transpose_tile_to(srcb, B_src, dstT, B_dstT, t0, banks, evac_engs):
            for g in range(2):
                bk = banks[g]
                for j in range(8):
                    c = g * 8 + j
                    S.op("pe", lambda e, c=c, j=j, bk=bk: e.transpose(psb16(bk)[:, j * 128:(j + 1) * 128], srcb[:, c * 128:(c + 1) * 128], ident),
                         reads=[B_src, B_const], writes=[PB[bk]], inc=(j == 7))
                dv = dstT[:, g * 8:(g + 1) * 8, t0:t0 + 128]
                sv = psb16(bk).rearrange("p (a b) -> p a b", a=8)
                if evac_engs[g] == "act":
                    S.op("act", lambda e, dv=dv, sv=sv: e.activation(dv, sv, AF.Copy), reads=[PB[bk]], writes=[B_dstT[g]])
                else:
                    S.op("dve", lambda e, dv=dv, sv=sv: e.tensor_copy(dv, sv), reads=[PB[bk]], writes=[B_dstT[g]])

        for tt in range(NX):
            S.dma("pool", xts[tt], x[tt * 128:(tt + 1) * 128, :], writes=[B_xt[tt]])
        if stop != "da1":
            for i in range(5):
                load_slab(i, [0, 1024, 4096, 2048, 3072][i])
        for tt in range(17):
            if tt < 16:
                s_ = tt % NX
                B_st = Buf("st1_%d" % tt)
                rms_token_tile(xts[s_], B_xt[s_], w1bc, B_w1, ubs[tt % 3], B_ub[tt % 3], st1[:, tt, :], B_st)
                if tt + NX < 16:
                    S.dma("pool", xts[s_], x[(tt + NX) * 128:(tt + NX + 1) * 128, :], writes=[B_xt[s_]])
            if tt >= 1:
                t_ = tt - 1
                transpose_tile_to(ubs[t_ % 3], B_ub[t_ % 3], uT, B_uT, t_ * 128, (0, 1) if t_ % 2 == 0 else (2, 3), ("act", "dve"))

        check_stop("p1", lambda: [dump(uT[:, fc, :], fc * S_LEN, B_uT) for fc in range(16)])
        A.off = p2_off

        pj_rot = [0]

        def proj(i, evac):
            for tb in range(4):
                bk = pj_rot[0] % 4
                pj_rot[0] += 1
                for fc in range(16):
                    S.op("pe", lambda e, fc=fc, tb=tb, bk=bk: e.matmul(psb(bk), lhsT=slabs[i][:, fc, :], rhs=uT[:, fc, tb * 512:(tb + 1) * 512],
                                                                      start=(fc == 0), stop=(fc == 15)),
                         reads=[B_slab[i]] + B_uT, writes=[PB[bk]], inc=(fc == 15))
                evac(tb, bk)

        def tsl(tb):
            return slice(tb * 512, (tb + 1) * 512)

        q32 = A.alloc([S_LEN], BF16); B_q = Buf("q32")
        vtok = A.alloc([16, 128], BF16); B_vtok = Buf("vtok")
        sgate = A.alloc([S_LEN], BF16); B_sg = Buf("sgate")
        O32 = A.alloc([S_LEN], F32); B_O = Buf("O32")
        R = A.alloc([4, S_LEN], F32); B_R = [Buf("R%d" % i) for i in range(4)]
        RA, RB, RC, RD = R[:, 0, :], R[:, 1, :], R[:, 2, :], R[:, 3, :]
        Xv = A.alloc([2 * S_LEN], F32); B_X = Buf("X")
        Dq = A.alloc([1024], F32); B_Dq = Buf("Dq")
        qtTs = [A.alloc([S_LEN], BF16) for _ in range(2)]; B_qts = [Buf("qtT%d" % i) for i in range(2)]
        ktTs = [A.alloc([S_LEN], BF16) for _ in range(2)]; B_kts = [Buf("ktT%d" % i) for i in range(2)]
        ktok = A.alloc([16, 128], BF16); B_ktok = Buf("ktok")
        ATm = A.alloc([16, 128], BF16); B_AT = Buf("ATm")
        vT = ATm.rearrange("p a b -> p (a b)"); B_vT = B_AT
        Sst = A.alloc([128, 32], BF16); B_Sst = Buf("Sst")
        Dts = [A.alloc([32], F32) for _ in range(2)]; B_Dt = [Buf("Dt%d" % i) for i in range(2)]
        Dscs = [A.alloc([32], F32) for _ in range(2)]; B_Dsc = [Buf("Dsc%d" % i) for i in range(2)]
        B_ycat = Buf("ycat_hbm")
        hg_end = A.off

        HG_COL = [0, 1024, 4096, 2048, 3072]
        B_conv = [Buf("conv%d" % i) for i in range(8)]
        conv_list = []
        for cb in range(4):
            for hf in range(2):
                conv_list.append((wo_s[cb, hf],
                                  w_out[hf * 1024:(hf + 1) * 1024, cb * 512:(cb + 1) * 512].rearrange("(fc p) n -> p fc n", p=128)))
        for pr in range(22):
            for g, wsrc in enumerate((w_gate, w_up)):
                for hf in range(2):
                    conv_list.append((wgu_s[pr, hf, :, :, g, :],
                                      wsrc[hf * 1024:(hf + 1) * 1024, pr * 256:(pr + 1) * 256].rearrange("(fc p) n -> p fc n", p=128)))
        for cb in range(4):
            for (j0, j1) in ((0, 16), (16, 32), (32, 44)):
                conv_list.append((wd_s[cb, :, j0:j1, :],
                                  w_down[j0 * 128:j1 * 128, cb * 512:(cb + 1) * 512].rearrange("(j p) n -> p j n", p=128)))
        conv_pos = [0]

        def emit_conv(n):
            for _ in range(n):
                if conv_pos[0] >= len(conv_list):
                    return
                dst_, src_ = conv_list[conv_pos[0]]
                S.dma("pool", dst_, src_, writes=[B_conv[conv_pos[0] % 8]])
                conv_pos[0] += 1

        def hg_proj(h, i, evac, defer=False):
            proj(i, evac)
            if not defer:
                post_proj(h, i)

        def post_proj(h, i):
            if h + 1 < 8:
                load_slab(i, HG_COL[i] + (h + 1) * 128)
            elif i < 3 and stop is None:
                load_slab(i, [5120, 6144, 7168][i])
            if stop is None:
                emit_conv(3)

        def projA(h, d):
            def ev_f(tb, bk):
                S.op("act", lambda e: e.activation(RA[:, tsl(tb)], psb(bk), AF.Sigmoid), reads=[PB[bk]], writes=[B_R[0]])
                S.op("act", lambda e: e.activation(RB[:, tsl(tb)], psb(bk), AF.Sigmoid, scale=-1.0), reads=[PB[bk]], writes=[B_R[1]])
            hg_proj(h, 3 + d, ev_f, defer=True)

        def projQ(h):
            hg_proj(h, 0, lambda tb, bk: S.op("act", lambda e: e.activation(q32[:, tsl(tb)], psb(bk), AF.Silu), reads=[PB[bk]], writes=[B_q]))

        def projI(h):
            hg_proj(h, 1, lambda tb, bk: S.op("act", lambda e: e.activation(vT[:, tsl(tb)], psb(bk), AF.Copy), reads=[PB[bk]], writes=[B_vT]))
            for g in range(2):
                bk = 4 + g
                for j in range(8):
                    blk = g * 8 + j
                    S.op("pe", lambda e, blk=blk, j=j, bk=bk: e.transpose(psb16(bk)[:, j * 128:(j + 1) * 128], vT[:, blk * 128:(blk + 1) * 128], ident),
                         reads=[B_vT, B_const], writes=[PB[bk]], inc=(j == 7))
                S.op("dve", lambda e, g=g, bk=bk: e.tensor_copy(vtok[:, g * 8:(g + 1) * 8, :], psb16(bk).rearrange("p (a b) -> p a b", a=8)),
                     reads=[PB[bk]], writes=[B_vtok])

        def projG(h):
            hg_proj(h, 2, lambda tb, bk: S.op("act", lambda e: e.activation(sgate[:, tsl(tb)], psb(bk), AF.Silu), reads=[PB[bk]], writes=[B_sg]))

        def chainA_early(h, d):
            fwd = (d == 0)
            oml = omlv[:, d, h:h + 1]
            lb = lbv[:, d, h:h + 1]
            S.op("act", lambda e: e.activation(RA, RA, AF.Ln, bias=lb, scale=oml), reads=[B_R[0], B_c2], writes=[B_R[0]])
            post_proj(h, 3 + d)
            if fwd:
                S.op("dve", lambda e: e.tensor_tensor_scan(RC, mskf, RA, 0.0, ALU.mult, ALU.add), reads=[B_R[0], B_msk], writes=[B_R[2]])
            else:
                S.op("dve", lambda e: e.tensor_tensor_scan(RC[:, ::-1], mskf, RA[:, ::-1], 0.0, ALU.mult, ALU.add),
                     reads=[B_R[0], B_msk], writes=[B_R[2]])
            S.op("act", lambda e: e.activation(RA, RC, AF.Exp), reads=[B_R[2]], writes=[B_R[0]])
            S.op("act", lambda e: e.activation(RD, RC, AF.Exp, scale=-1.0), reads=[B_R[2]], writes=[B_R[3]])
            dsrc = RA[:, 63::64] if fwd else RA[:, 0::64]
            S.op("dve", lambda e: e.tensor_copy(Dts[d], dsrc), reads=[B_R[0]], writes=[B_Dt[d]])
            S.op("dve", lambda e: e.tensor_copy(Dscs[d], dsrc), reads=[B_R[0]], writes=[B_Dsc[d]])
            zc = 0 if fwd else 31
            S.op("dve", lambda e: e.memset(Dscs[d][:, zc:zc + 1], 0.0), writes=[B_Dsc[d]])

        def chainA_late(h, d):
            oml = omlv[:, d, h:h + 1]
            S.op("dve", lambda e: e.tensor_tensor(qtTs[d], q32, RA, ALU.mult), reads=[B_q, B_R[0]], writes=[B_qts[d]])
            S.op("dve", lambda e: e.scalar_tensor_tensor(ktTs[d], RB, oml, RD, ALU.mult, ALU.mult), reads=[B_R[1], B_R[3], B_c2], writes=[B_kts[d]])

        def B_pre(h, d):
            fwd = (d == 0)
            Dt, Dsc = Dts[d], Dscs[d]
            qtT, ktT, B_qt, B_kt = qtTs[d], ktTs[d], B_qts[d], B_kts[d]
            S.op("dve", lambda e: e.tensor_copy(Dq.rearrange("p (v c) -> p v c", c=32), Dsc.unsqueeze(1).to_broadcast([128, 32, 32])),
                 reads=[B_Dsc[d]], writes=[B_Dq])
            for g in range(2):
                bk = 4 + g
                for j in range(8):
                    blk = g * 8 + j
                    S.op("pe", lambda e, blk=blk, j=j, bk=bk: e.transpose(psb16(bk)[:, j * 128:(j + 1) * 128], ktT[:, blk * 128:(blk + 1) * 128], ident),
                         reads=[B_kt, B_const], writes=[PB[bk]], inc=(j == 7))
                S.op("act", lambda e, g=g, bk=bk: e.activation(ktok[:, g * 8:(g + 1) * 8, :], psb16(bk).rearrange("p (a b) -> p a b", a=8), AF.Copy),
                     reads=[PB[bk]], writes=[B_ktok])
            msk = maskF if fwd else maskB
            for g in range(4):
                bk = 6 + (g % 2)
                for j in range(4):
                    blk = g * 4 + j
                    S.op("pe", lambda e, blk=blk, j=j, bk=bk: e.matmul(psb(bk)[:, j * 128:(j + 1) * 128], lhsT=ktT[:, blk * 128:(blk + 1) * 128],
                                                                     rhs=qtT[:, blk * 128:(blk + 1) * 128], start=True, stop=True),
                         reads=[B_kt, B_qt], writes=[PB[bk]], inc=(j == 3))
                S.op("dve", lambda e, g=g, bk=bk: e.tensor_tensor(ATm[:, g * 4:(g + 1) * 4, :].rearrange("p a b -> p (a b)"), psb(bk), msk, ALU.mult),
                     reads=[PB[bk], B_ldp], writes=[B_AT])
            Xr = Xv.rearrange("p (v c) -> p c v", c=32)
            for g in range(4):
                for half in range(2):
                    bk = 4 + half + 2 * (g % 2)
                    for j in range(4):
                        blk = g * 4 + j
                        S.op("pe", lambda e, blk=blk, half=half, j=j, bk=bk: e.matmul(psb(bk)[:, j * 128:(j + 1) * 128],
                                                                                       lhsT=ktok[half * 64:(half + 1) * 64, blk, :],
                                                                                       rhs=vtok[half * 64:(half + 1) * 64, blk, :], start=True, stop=True),
                             reads=[B_ktok, B_vtok], writes=[PB[bk]], inc=(j == 3))
                for half in range(2):
                    bk = 4 + half + 2 * (g % 2)
                    c0 = g * 8 + half
                    S.op("dve", lambda e, c0=c0, bk=bk: e.tensor_tensor(Xr[:, c0:c0 + 7:2, :], psb(bk).rearrange("p (a b) -> p a b", a=4),
                                                                       Dt[:, c0:c0 + 7:2].unsqueeze(2).to_broadcast([128, 4, 128]), ALU.mult),
                         reads=[PB[bk], B_Dt[d]], writes=[B_X])
            Sflat = Sst.rearrange("p v c -> p (v c)")
            for vq in range(4):
                sl = slice(vq * 1024, (vq + 1) * 1024)
                if fwd:
                    S.op("dve", lambda e, sl=sl: e.tensor_tensor_scan(Sflat[:, sl], Dq, Xv[:, sl], 0.0, ALU.mult, ALU.add),
                         reads=[B_X, B_Dq], writes=[B_Sst])
                else:
                    S.op("dve", lambda e, sl=sl: e.tensor_tensor_scan(Sflat[:, sl][:, ::-1], Dq[:, ::-1], Xv[:, sl][:, ::-1], 0.0, ALU.mult, ALU.add),
                         reads=[B_X, B_Dq], writes=[B_Sst])

        def OT(h, d):
            fwd = (d == 0)
            qtT, B_qt = qtTs[d], B_qts[d]
            for tb in range(4):
                bk = tb
                for j in range(4):
                    blk = tb * 4 + j
                    mm = []
                    mm.append((psb(bk)[:, j * 128:(j + 1) * 128], vtok[:, blk, :], ATm[:, blk, :], [B_vtok, B_AT]))
                    for half in range(2):
                        c = blk * 2 + half
                        cp = c - 1 if fwd else c + 1
                        if cp < 0 or cp > 31:
                            continue
                        mm.append((psb(bk)[:, j * 128 + half * 64: j * 128 + (half + 1) * 64], Sst[:, :, cp], qtT[:, c * 64:(c + 1) * 64], [B_Sst, B_qt]))
                    for i, (o_, l_, r_, rd) in enumerate(mm):
                        S.op("pe", lambda e, o_=o_, l_=l_, r_=r_, i=i, n=len(mm): e.matmul(o_, lhsT=l_, rhs=r_, start=(i == 0), stop=(i == n - 1)),
                             reads=rd, writes=[PB[bk]], inc=(j == 3 and i == len(mm) - 1))
                if fwd:
                    S.op("act", lambda e, tb=tb, bk=bk: e.activation(O32[:, tsl(tb)], psb(bk), AF.Copy), reads=[PB[bk]], writes=[B_O])
                else:
                    S.op("dve", lambda e, tb=tb, bk=bk: e.tensor_tensor(O32[:, tsl(tb)], O32[:, tsl(tb)], psb(bk), ALU.add), reads=[PB[bk], B_O], writes=[B_O])

        def outnorm(h):
            sq = ATm.rearrange("p a b -> p (a b)")
            rs = Xv[:, 0:S_LEN]
            yst = ktok.rearrange("p a b -> p (a b)")
            S.op("act", lambda e: e.activation(sq, O32, AF.Square), reads=[B_O], writes=[B_AT])
            for tb in range(4):
                bk = 4 + tb
                S.op("pe", lambda e, tb=tb, bk=bk: e.matmul(psb(bk), lhsT=ones_bf, rhs=sq[:, tsl(tb)], start=True, stop=True),
                     reads=[B_AT, B_const], writes=[PB[bk]], inc=True)
                S.op("act", lambda e, tb=tb, bk=bk: e.activation(rs[:, tsl(tb)], psb(bk), AF.Ln, bias=epsc, scale=1.0 / 128), reads=[PB[bk], B_const], writes=[B_X])
            S.op("act", lambda e: e.activation(rs, rs, AF.Exp, scale=-0.5), reads=[B_X], writes=[B_X])
            S.op("dve", lambda e: e.tensor_tensor(O32, O32, rs, ALU.mult), reads=[B_O, B_X], writes=[B_O])
            S.op("dve", lambda e: e.scalar_tensor_tensor(yst, O32, onw, sgate, ALU.mult, ALU.mult), reads=[B_O, B_sg, B_ld], writes=[B_ktok])
            S.dma("sp", ycat[h * 128:(h + 1) * 128, :], yst, reads=[B_ktok], writes=[B_ycat], slot=B_ktok)

        def hg_all():
            projQ(0); projA(0, 0); chainA_early(0, 0); chainA_late(0, 0)
            for h in range(8):
                projI(h)
                projA(h, 1)
                B_pre(h, 0)
                chainA_early(h, 1)
                chainA_late(h, 1)
                projG(h)
                OT(h, 0)
                B_pre(h, 1)
                if h + 1 < 8:
                    projQ(h + 1)
                    projA(h + 1, 0)
                OT(h, 1)
                outnorm(h)
                if h + 1 < 8:
                    chainA_early(h + 1, 0)
                    chainA_late(h + 1, 0)
                if h == 0:
                    check_stop("hg1", lambda: [dump(ktok.rearrange("p a b -> p (a b)"), 0, [B_ktok])])

        def da_heads():
            A.off = hd_off - 2 * 16 * 128
            qTa = [A.alloc([S_LEN], BF16) for _ in range(2)]; qTb = [A.alloc([S_LEN], BF16) for _ in range(2)]
            kT = [A.alloc([S_LEN], BF16) for _ in range(2)]
            vTd = A.alloc([S_LEN], BF16)
            Vx = A.alloc([16, 3, 130], BF16)
            Pt = [A.alloc([16, 1024], BF16) for _ in range(2)]
            ycTd = A.alloc([S_LEN], BF16)
            NG = 4
            o32s = [A.alloc([128], F32) for _ in range(NG)]
            t2s = [A.alloc([128], F32) for _ in range(NG)]
            ybfs = [A.alloc([128], BF16) for _ in range(NG)]
            sts = A.alloc([NG, 8], F32)
            mhalf = A.alloc([1], F32)
            B_qT = [Buf("dqT0"), Buf("dqT1")]; B_kT = [Buf("dkT0"), Buf("dkT1")]
            B_vTd, B_Vx, B_ycTd = Buf("dvT"), Buf("Vx"), Buf("dycT")
            B_P = [[Buf("P%d_%d" % (i, kb)) for kb in range(16)] for i in range(2)]
            B_g = [Buf("grp%d" % i) for i in range(NG)]
            B_mh = Buf("mhalf")
            S.op("pool", lambda e: e.memset(Vx[:, :, 0, 128:129], 1.0), writes=[B_Vx])
            for p_ in range(2):
                S.op("pool", lambda e, p_=p_: e.memset(qTa[p_][64:128, :], 0.0), writes=[B_qT[p_]])
                S.op("pool", lambda e, p_=p_: e.memset(qTb[p_][0:64, :], 0.0), writes=[B_qT[p_]])
            S.op("pool", lambda e: e.memset(mhalf, -0.5), writes=[B_mh])
            if stop is not None:
                load_slab(0, 5120); load_slab(1, 6144); load_slab(2, 7168)
            cnt = [0]
            gcnt = [0]
            DA_COL = [5120, 6144, 7168]

            def proj_units(h):
                p_ = h % 2
                lst = []
                for si in range(3):
                    for tb in range(4):
                        def u(si=si, tb=tb):
                            pb_ = 4 + (si * 4 + tb) % 4
                            for fc in range(16):
                                S.op("pe", lambda e, fc=fc: e.matmul(psb(pb_), lhsT=slabs[si][:, fc, :], rhs=uT[:, fc, tb * 512:(tb + 1) * 512],
                                                                     start=(fc == 0), stop=(fc == 15)),
                                     reads=[B_slab[si]] + B_uT, writes=[PB[pb_]], inc=(fc == 15))
                            if si == 0:
                                S.op("dve", lambda e: e.tensor_scalar(qTa[p_][0:64, tsl(tb)], psb(pb_)[0:64, :], 0.125, None, ALU.mult), reads=[PB[pb_]], writes=[B_qT[p_]])
                                S.op("dve", lambda e: e.tensor_scalar(qTb[p_][64:128, tsl(tb)], psb(pb_)[64:128, :], 0.125, None, ALU.mult), reads=[PB[pb_]], writes=[B_qT[p_]])
                            elif si == 1:
                                S.op("dve", lambda e: e.tensor_copy(kT[p_][:, tsl(tb)], psb(pb_)), reads=[PB[pb_]], writes=[B_kT[p_]])
                            else:
                                S.op("dve", lambda e: e.tensor_copy(vTd[:, tsl(tb)], psb(pb_)), reads=[PB[pb_]], writes=[B_vTd])
                            if tb == 3:
                                if h + 1 < 8:
                                    load_slab(si, DA_COL[si] + (h + 1) * 128)
                                elif si == 2 and stop is None:
                                    p3_early_loads(False)
                        lst.append(u)
                return lst

            for u in proj_units(0):
                u()
            for h in range(8):
                p_ = h % 2
                if h == 7 and stop is None:
                    p3_early_ycs()
                eA = efar[:, 2 * h:2 * h + 1]
                eB = efar[:, 2 * h + 1:2 * h + 2]
                for g in range(2):
                    bk = 6 + g
                    for j in range(8):
                        blk = g * 8 + j
                        S.op("pe", lambda e, blk=blk, j=j, bk=bk: e.transpose(psb16(bk)[:, j * 128:(j + 1) * 128], vTd[:, blk * 128:(blk + 1) * 128], ident),
                             reads=[B_vTd, B_const], writes=[PB[bk]], inc=(j == 7))
                    src = psb16(bk).rearrange("p (a b) -> p a b", a=8)
                    S.op("dve", lambda e, g=g, src=src: e.tensor_copy(Vx[:, g * 8:(g + 1) * 8, 0, 0:128], src), reads=[PB[bk]], writes=[B_Vx])
                    S.op("act", lambda e, g=g, src=src: e.activation(Vx[:, g * 8:(g + 1) * 8, 1, 0:128], src, AF.Copy, scale=eA), reads=[PB[bk], B_c2], writes=[B_Vx])
                    S.op("act", lambda e, g=g, src=src: e.activation(Vx[:, g * 8:(g + 1) * 8, 2, 0:128], src, AF.Copy, scale=eB), reads=[PB[bk], B_c2], writes=[B_Vx])
                S.op("dve", lambda e: e.tensor_copy(Vx[:, :, 1, 128:129], eA.unsqueeze(1).to_broadcast([128, 16, 1])), reads=[B_c2], writes=[B_Vx])
                S.op("dve", lambda e: e.tensor_copy(Vx[:, :, 2, 128:129], eB.unsqueeze(1).to_broadcast([128, 16, 1])), reads=[B_c2], writes=[B_Vx])

                def qk(qc, kb, pi):
                    sp_ = (cnt[0] % 2) * 2
                    cnt[0] += 1
                    for m, qsrc in enumerate((qTa[p_], qTb[p_])):
                        S.op("pe", lambda e, m=m, qsrc=qsrc, sp_=sp_: e.matmul(psb(sp_ + m), lhsT=kT[p_][:, kb * 128:(kb + 1) * 128],
                                                                                rhs=qsrc[:, qc * 512:(qc + 1) * 512], start=True, stop=True),
                             reads=[B_kT[p_], B_qT[p_]], writes=[PB[sp_ + m]], inc=(m == 1))
                    S.op("act", lambda e, sp_=sp_: e.activation(Pt[pi][:, kb, :], psb(sp_, 2), AF.Exp), reads=[PB[sp_], PB[sp_ + 1]], writes=[B_P[pi][kb]])
                    lo = max(kb - 1, 4 * qc)
                    hi = min(kb + 1, 4 * qc + 3)
                    if lo <= hi:
                        n = (hi - lo + 1) * 128
                        c0 = (lo - 4 * qc) * 128
                        e0 = (lo - (kb - 1)) * 128
                        pv_ = Pt[pi][:, kb, :].rearrange("p (m q) -> p m q", m=2)[:, :, c0:c0 + n]
                        ev_ = Eall[:, h, e0:e0 + n].unsqueeze(1).to_broadcast([128, 2, n])
                        S.op("dve", lambda e, pv_=pv_, ev_=ev_: e.tensor_tensor(pv_, pv_, ev_, ALU.mult), reads=[B_c2], writes=[B_P[pi][kb]])

                fin_q = []

                def pv_steps(qc, qb, pi):
                    qbg = 4 * qc + qb
                    gi = gcnt[0] % NG
                    gcnt[0] += 1
                    accs = []
                    mms = []
                    for m in range(2):
                        a = (qb % 2) * 2 + m
                        bk = 4 + a // 2
                        col = (a % 2) * 130
                        acc = psb(bk)[:, col:col + 129]
                        accs.append((acc, bk))
                        for kb in range(16):
                            var = 1 if kb > qbg + 1 else (2 if kb < qbg - 1 else 0)
                            mms.append((acc, bk, kb, var, m))

                    def emit_mm(lst):
                        for (acc, bk, kb, var, m) in lst:
                            S.op("pe", lambda e, acc=acc, kb=kb, var=var, m=m: e.matmul(acc, lhsT=Pt[pi][:, kb, m * 512 + qb * 128: m * 512 + (qb + 1) * 128],
                                                                                          rhs=Vx[:, kb, var, 0:129], start=(kb == 0), stop=(kb == 15)),
                                 reads=[B_P[pi][kb], B_Vx], writes=[PB[bk]], inc=(kb == 15))

                    def chain():
                        (a1, b1), (a2, b2) = accs
                        stq = sts[:, gi, :]
                        o32, t2, ybf, Bg = o32s[gi], t2s[gi], ybfs[gi], B_g[gi]
                        S.op("dve", lambda e: e.reciprocal(stq[:, 0:1], a1[:, 128:129]), reads=[PB[b1]], writes=[Bg])
                        S.op("dve", lambda e: e.reciprocal(stq[:, 1:2], a2[:, 128:129]), reads=[PB[b2]], writes=[Bg])
                        S.op("dve", lambda e: e.tensor_tensor(stq[:, 2:3], stq[:, 1:2], neglam, ALU.mult), reads=[Bg, B_c2], writes=[Bg])
                        S.op("dve", lambda e: e.tensor_scalar(t2, a2[:, 0:128], stq[:, 2:3], None, ALU.mult), reads=[PB[b2], Bg], writes=[Bg])
                        S.op("dve", lambda e: e.scalar_tensor_tensor(o32, a1[:, 0:128], stq[:, 0:1], t2, ALU.mult, ALU.add), reads=[PB[b1], Bg], writes=[Bg])
                        S.op("dve", lambda e: e.scalar_tensor_tensor(t2, o32, 1.0, o32, ALU.mult, ALU.mult, accum_out=stq[:, 3:4]), reads=[Bg], writes=[Bg])
                        S.op("pool", lambda e: e.tensor_scalar(stq[:, 4:5], stq[:, 3:4], 1.0 / 128, EPS, ALU.mult, ALU.add), reads=[Bg], writes=[Bg])
                        S.op("pool", lambda e: e.tensor_tensor(stq[:, 5:6], stq[:, 4:5], mhalf, ALU.pow), reads=[Bg, B_mh], writes=[Bg])
                        S.op("dve", lambda e: e.scalar_tensor_tensor(ybf, o32, stq[:, 5:6], sublnbc, ALU.mult, ALU.mult), reads=[Bg, B_c2], writes=[Bg])

                        def fin():
                            tv = psb16(6)[:, (gi % 2) * 128:(gi % 2 + 1) * 128]
                            S.op("pe", lambda e: e.transpose(tv, ybf, ident), reads=[Bg, B_const], writes=[PB[6]], inc=True)
                            S.op("dve", lambda e: e.tensor_copy(ycTd[:, qbg * 128:(qbg + 1) * 128], tv), reads=[PB[6]], writes=[B_ycTd])
                        fin_q.append(fin)

                    steps = []
                    for i in range(4):
                        part = mms[i * 8:(i + 1) * 8]
                        if i < 3:
                            steps.append(lambda part=part: emit_mm(part))
                        else:
                            def last(part=part):
                                emit_mm(part)
                                while len(fin_q) > 0:
                                    fin_q.pop(0)()
                                chain()
                            steps.append(last)
                    return steps

                nxt = proj_units(h + 1) if h + 1 < 8 else []
                for qc in range(5):
                    steps = []
                    if qc > 0:
                        for qb in range(4):
                            steps += pv_steps(qc - 1, qb, (qc - 1) % 2)
                    for kb in range(16):
                        if qc < 4:
                            qk(qc, kb, qc % 2)
                        if qc == 0 and len(nxt) > 0:
                            nxt.pop(0)()
                        if qc > 0:
                            steps[kb]()
                while len(nxt) > 0:
                    nxt.pop(0)()
                while len(fin_q) > 0:
                    fin_q.pop(0)()
                S.dma("sp", ycat[1024 + h * 128:1024 + (h + 1) * 128, :], ycTd, reads=[B_ycTd], writes=[B_ycat], slot=B_ycTd)
                if h == 0:
                    check_stop("da1", lambda: [dump(ycTd, 0, [B_ycTd]), dump(kT[0], 2 * S_LEN, [B_kT[0]]), dump(vTd, 3 * S_LEN, [B_vTd])])

        if stop != "da1":
            hg_all()
        if stop is None:
            emit_conv(1000)
        S.barrier()
        _save = A.off
        A.off = base_off
        w2bc = A.alloc([DM], F32); wfbc = A.alloc([DM], F32); B_w23 = Buf("w23")
        hx = A.alloc([4, DM], F32); B_hx = [Buf("hx%d" % i) for i in range(4)]
        ycs = A.alloc([16, 512], BF16); B_ycs = Buf("ycs")
        p3_rest_off = A.off
        assert p3_rest_off <= base_off + 16 * S_LEN, "phase-3 early buffers must fit inside uT"
        A.off = _save

        def load_x(ch, tt, extra=()):
            t0_ = ch * 512 + tt * 128
            S.dma("pool", hx[:, tt, :], x[t0_:t0_ + 128, :], writes=[B_hx[tt]] + list(extra))

        def p3_early_loads(with_ycs=True):
            S.dma("pool", w2bc, norm2_w.partition_broadcast(128), writes=[B_w23] + B_uT)
            S.dma("pool", wfbc, final_w.partition_broadcast(128), writes=[B_w23] + B_uT)
            for tt in range(4):
                load_x(0, tt, extra=B_uT)
            if with_ycs:
                p3_early_ycs()

        def p3_early_ycs():
            S.dma("pool", ycs[:, 0:15, :], ycat[0:1920, 0:512].rearrange("(c p) t -> p c t", p=128), reads=[B_ycat], writes=[B_ycs] + B_uT)

        da_heads()
        S.barrier()

        A.off = p3_rest_off
        u2T = A.alloc([16, 512], BF16); B_u2T = [Buf("u2Ta"), Buf("u2Tb")]
        hid = A.alloc([NJ, 512], BF16); B_hid = Buf("hid")
        NU = 5
        units = [A.alloc([4096], BF16) for _ in range(NU)]; B_un = [Buf("unit%d" % i) for i in range(NU)]
        ub3 = [A.alloc([DM], BF16) for _ in range(2)]; B_ub3 = [Buf("ub3_%d" % i) for i in range(2)]
        sgs = [A.alloc([512], F32) for _ in range(2)]; B_sgs = [Buf("sgs%d" % i) for i in range(2)]
        ob = [A.alloc([DM], F32) for _ in range(2)]; B_ob = [Buf("ob%d" % i) for i in range(2)]
        st3 = A.alloc([64, 4], F32)
        B_out = Buf("out_hbm")
        urot = [0]

        def load_unit(src_ap, nel):
            i = urot[0] % NU
            urot[0] += 1
            S.dma("sp", units[i][:, 0:nel], src_ap, writes=[B_un[i]])
            return units[i], B_un[i]

        def rms3(src, B_src, wbc, dst, B_dst, stc):
            rms_token_tile(src, B_src, wbc, B_w23, dst, B_dst, stc, Buf("st3"))

        def load_ycs(ch):
            S.dma("pool", ycs, ycat[:, ch * 512:(ch + 1) * 512].rearrange("(c p) t -> p c t", p=128), reads=[B_ycat], writes=[B_ycs])

        S.dma("pool", ycs[:, 15, :], ycat[1920:2048, 0:512], reads=[B_ycat], writes=[B_ycs])
        for ch in range(4):
            t0 = ch * 512
            for cb in range(4):
                bks = [(4 if cb % 2 == 0 else 0) + tt for tt in range(4)]
                for hf in range(2):
                    un, Bu = load_unit(wo_s[cb, hf].rearrange("p fc n -> p (fc n)"), 4096)
                    uv = un.rearrange("p (fc n) -> p fc n", fc=8)
                    for tt in range(4):
                        for fc in range(8):
                            S.op("pe", lambda e, fc=fc, tt=tt, hf=hf, uv=uv: e.matmul(psb(bks[tt]), lhsT=ycs[:, hf * 8 + fc, tt * 128:(tt + 1) * 128], rhs=uv[:, fc, :],
                                                                                       start=(hf == 0 and fc == 0), stop=(hf == 1 and fc == 7)),
                                 reads=[B_ycs, Bu], writes=[PB[bks[tt]]], inc=(fc == 7))
                for tt in range(4):
                    hv = hx[:, tt, cb * 512:(cb + 1) * 512]
                    S.op("dve", lambda e, hv=hv, bk=bks[tt]: e.tensor_tensor(hv, hv, psb(bk), ALU.add), reads=[PB[bks[tt]], B_hx[tt]], writes=[B_hx[tt]])
            if ch + 1 < 4:
                load_ycs(ch + 1)
            for tt in range(5):
                if tt < 4:
                    rms3(hx[:, tt, :], B_hx[tt], w2bc, ub3[tt % 2], B_ub3[tt % 2], st3[:, ch * 8 + tt, :])
                if tt >= 1:
                    t_ = tt - 1
                    transpose_tile_to(ub3[t_ % 2], B_ub3[t_ % 2], u2T, B_u2T, t_ * 128, (0, 1) if t_ % 2 == 0 else (2, 3), ("act", "dve"))
            for pr in range(NJ // 2):
                base = 4 if pr % 2 == 0 else 0
                for hf in range(2):
                    un, Bu = load_unit(wgu_s[pr, hf].rearrange("p fc g n -> p (fc g n)"), 4096)
                    uv = un.rearrange("p (fc g n) -> p fc g n", fc=8, g=2)
                    for jj in range(2):
                        for g in range(2):
                            bk = base + jj * 2 + g
                            for fc in range(8):
                                S.op("pe", lambda e, fc=fc, g=g, jj=jj, bk=bk, hf=hf, uv=uv: e.matmul(psb(bk), lhsT=uv[:, fc, g, jj * 128:(jj + 1) * 128],
                                                                                                       rhs=u2T[:, hf * 8 + fc, :],
                                                                                                       start=(hf == 0 and fc == 0), stop=(hf == 1 and fc == 7)),
                                     reads=[Bu] + B_u2T, writes=[PB[bk]], inc=(fc == 7))
                for jj in range(2):
                    j = pr * 2 + jj
                    k = j % 2
                    bg, bu = base + jj * 2, base + jj * 2 + 1
                    S.op("act", lambda e, k=k, bg=bg: e.activation(sgs[k], psb(bg), AF.Silu), reads=[PB[bg]], writes=[B_sgs[k]])
                    S.op("dve", lambda e, k=k, bu=bu, j=j: e.tensor_tensor(hid[:, j, :], sgs[k], psb(bu), ALU.mult), reads=[B_sgs[k], PB[bu]], writes=[B_hid])
            for cb in range(4):
                bks = [(4 if cb % 2 == 0 else 0) + tt for tt in range(4)]
                for (j0, nj) in ((0, 8), (8, 8), (16, 8), (24, 8), (32, 8), (40, 4)):
                    un, Bu = load_unit(wd_s[cb, :, j0:j0 + nj, :].rearrange("p j n -> p (j n)"), nj * 512)
                    uv = un[:, 0:nj * 512].rearrange("p (j n) -> p j n", j=nj)
                    for tt in range(4):
                        for jl in range(nj):
                            j = j0 + jl
                            S.op("pe", lambda e, j=j, jl=jl, tt=tt, uv=uv: e.matmul(psb(bks[tt]), lhsT=hid[:, j, tt * 128:(tt + 1) * 128], rhs=uv[:, jl, :],
                                                                                     start=(j == 0), stop=(j == NJ - 1)),
                                 reads=[B_hid, Bu], writes=[PB[bks[tt]]], inc=(jl == nj - 1))
                for tt in range(4):
                    hv = hx[:, tt, cb * 512:(cb + 1) * 512]
                    S.op("dve", lambda e, hv=hv, bk=bks[tt]: e.tensor_tensor(hv, hv, psb(bk), ALU.add), reads=[PB[bks[tt]], B_hx[tt]], writes=[B_hx[tt]])
            for tt in range(4):
                s_ = tt % 2
                rms3(hx[:, tt, :], B_hx[tt], wfbc, ob[s_], B_ob[s_], st3[:, ch * 8 + 4 + tt, :])
                S.dma("pool", out[t0 + tt * 128:t0 + (tt + 1) * 128, :], ob[s_], reads=[B_ob[s_]], writes=[B_out], slot=B_ob[s_])
                if ch + 1 < 4:
                    load_x(ch + 1, tt)

        S.barrier()
        S.emit()
    return nc


def _rel_bucket_table():
    N_BUCKETS, MAX_DISTANCE = 32, 128
    nb = N_BUCKETS // 2
    max_exact = nb // 2
    try:
        import jax
        import jax.numpy as jnp

        def rel_bucket(rel):
            ret = jnp.where(rel > 0, nb, 0)
            n = jnp.abs(rel)
            nf = jnp.maximum(n, 1).astype(jnp.float32)
            large = max_exact + (jnp.log(nf / max_exact) / math.log(MAX_DISTANCE / max_exact)
                                 * (nb - max_exact)).astype(jnp.int32)
            large = jnp.minimum(large, nb - 1)
            return ret + jnp.where(n < max_exact, n, large)

        with jax.default_device(jax.devices("cpu")[0]):
            return np.asarray(rel_bucket(jnp.arange(-255, 256, dtype=jnp.int32)))
    except Exception:
        rel = np.arange(-255, 256, dtype=np.int32)
        ret = np.where(rel > 0, nb, 0)
        n = np.abs(rel)
        nf = np.maximum(n, 1).astype(np.float32)
        large = max_exact + (np.log(nf / np.float32(max_exact)) / np.float32(math.log(MAX_DISTANCE / max_exact))
                             * np.float32(nb - max_exact)).astype(np.int32)
        large = np.minimum(large, nb - 1)
        return ret + np.where(n < max_exact, n, large)


_NC_CACHE = {}


def kernel(x, norm1_w, w_in, hg_lb_logits, hg_onorm_w, lambda_q1, lambda_k1, lambda_q2, lambda_k2,
           da_subln_w, rel_bias, w_out, norm2_w, w_gate, w_up, w_down, final_norm_w):
    f32 = lambda a: np.ascontiguousarray(np.asarray(a, dtype=np.float32))
    x = f32(x)
    rel_bias = f32(rel_bias)
    bucket = _rel_bucket_table()
    kk = np.arange(128)[:, None]
    qq = np.arange(384)[None, :]
    rel = kk - (qq - 128)
    bidx = bucket[rel + 255]
    bias_near = np.ascontiguousarray(np.transpose(rel_bias[bidx], (0, 2, 1))).reshape(128, 8 * 384)
    bias_far = np.ascontiguousarray(np.stack([rel_bias[31, :], rel_bias[15, :]], axis=1)).reshape(16)
    s_ = np.arange(128)[:, None]
    t_ = np.arange(128)[None, :]
    same = (s_ // 64) == (t_ // 64)
    mF = (same & (s_ <= t_)).astype(np.float32)
    mB = (same & (s_ >= t_)).astype(np.float32)
    hmask = np.ascontiguousarray(np.concatenate([np.tile(mF, (1, 4)), np.tile(mB, (1, 4))], axis=1))
    lam_in = np.ascontiguousarray(np.stack([f32(lambda_q1)[0], f32(lambda_k1)[0], f32(lambda_q2)[0], f32(lambda_k2)[0]], axis=0))
    shared = {
        "w_in": f32(w_in)[0], "w_out": f32(w_out)[0], "w_gate": f32(w_gate)[0], "w_up": f32(w_up)[0], "w_down": f32(w_down)[0],
        "norm1_w": f32(norm1_w)[0], "norm2_w": f32(norm2_w)[0], "final_w": f32(final_norm_w),
        "lb_logits": f32(hg_lb_logits), "onorm_w": f32(hg_onorm_w)[0], "subln_w": f32(da_subln_w)[0],
        "lam_in": lam_in, "bias_near": bias_near, "bias_far": bias_far, "hmask": hmask,
    }
    if "nc" not in _NC_CACHE:
        _NC_CACHE["nc"] = build_nc()
    nc = _NC_CACHE["nc"]
    n = x.shape[0]
    in_maps = [dict(shared, x=x[b]) for b in range(n)]
    res = run_bass_kernel_spmd(nc, in_maps, core_ids=list(range(n)))
    return np.stack([np.asarray(r["out"], dtype=np.float32) for r in res.results], axis=0)
```
